# Optimizing a Trainium2 kernel written in Bass

```python
import jax
import jax.numpy as jnp
from jax import lax
import numpy as np

D_MODEL = 2048
BATCH = 2
SEQ = 16384
DEPTH = 2

EPS = 1e-6
NEG_INF = -1e30
ROPE_THETA = 500000.0

ML_HEADS = 4
ML_DK = 64
ML_DV = 128
ML_CHUNK = 64
ML_W = ML_HEADS * ML_DV

MLA_HEADS = 6
MLA_Q_RANK = 384
MLA_KV_RANK = 128
MLA_NOPE = 128
MLA_ROPE = 64
MLA_V = 128
MLA_Q_BLOCK = 128
MLA_W = MLA_HEADS * MLA_V

DIL_HEADS = 6
DIL_DH = 128
DIL_ROT = DIL_DH // 4
DIL_PAIRS = ((128, 1), (512, 4), (2048, 16))
DIL_W = DIL_HEADS * DIL_DH

D_MIX = ML_W + MLA_W + DIL_W
D_FF = 4 * D_MODEL

IN_SPLITS = (ML_HEADS * ML_DK, ML_HEADS * ML_DK, ML_W, ML_W, 4 * ML_HEADS,
             MLA_Q_RANK, MLA_KV_RANK + MLA_ROPE, DIL_W, DIL_W, DIL_W)
D_IN = sum(IN_SPLITS)

kernel_name = 'hybrid_parallel_mixer_encoder'


def rms_norm(x, gain):
    x32 = x.astype(jnp.float32)
    y = x32 * lax.rsqrt(jnp.mean(x32 * x32, axis=-1, keepdims=True) + EPS)
    return (y * gain.astype(jnp.float32)).astype(x.dtype)


def rope_tables(seq, rot_dim):
    pos = jnp.arange(seq, dtype=jnp.float32)
    inv_freq = ROPE_THETA ** (-jnp.arange(0, rot_dim, 2, dtype=jnp.float32) / rot_dim)
    ang = pos[:, None] * inv_freq[None, :]
    return jnp.cos(ang), jnp.sin(ang)


def apply_rope(x, cos, sin):
    x1, x2 = jnp.split(x.astype(jnp.float32), 2, axis=-1)
    return jnp.concatenate([x1 * cos - x2 * sin, x2 * cos + x1 * sin], axis=-1).astype(x.dtype)


def mlstm_chunkwise(q, k, v, i_pre, logf):
    N, H, S, dk = q.shape
    dv = v.shape[-1]
    L = ML_CHUNK
    nc = S // L

    def to_chunks(a):
        return jnp.moveaxis(a.reshape(a.shape[:2] + (nc, L) + a.shape[3:]), 2, 0)

    xs = (to_chunks(q), to_chunks(k), to_chunks(v), to_chunks(i_pre), to_chunks(logf))
    tril = jnp.tril(jnp.ones((L, L), dtype=bool))

    def step(carry, inp):
        C, n, m = carry
        qb, kb, vb, ib, fb = inp
        b = jnp.cumsum(fb, axis=-1)
        d_intra = jnp.where(tril, b[..., :, None] - b[..., None, :] + ib[..., None, :], NEG_INF)
        d_inter = b + m[..., None]
        m_t = jnp.maximum(d_inter, jnp.max(d_intra, axis=-1))
        w_intra = jnp.exp(d_intra - m_t[..., None])
        w_inter = jnp.exp(d_inter - m_t)
        s = jnp.einsum('nhtd,nhsd->nhts', qb, kb) * w_intra
        num = jnp.einsum('nhts,nhsv->nhtv', s, vb) + w_inter[..., None] * jnp.einsum('nhvd,nhtd->nhtv', C, qb)
        den = jnp.sum(s, axis=-1) + w_inter * jnp.einsum('nhd,nhtd->nht', n, qb)
        h = num / jnp.maximum(jnp.abs(den), jnp.exp(-m_t))[..., None]
        b_last = b[..., -1]
        d_state = b_last[..., None] - b + ib
        m_new = jnp.maximum(b_last + m, jnp.max(d_state, axis=-1))
        w_s = jnp.exp(d_state - m_new[..., None])
        w_c = jnp.exp(b_last + m - m_new)
        C_new = w_c[..., None, None] * C + jnp.einsum('nhsv,nhsd->nhvd', vb * w_s[..., None], kb)
        n_new = w_c[..., None] * n + jnp.einsum('nhs,nhsd->nhd', w_s, kb)
        return (C_new, n_new, m_new), h

    init = (jnp.zeros((N, H, dv, dk), jnp.float32), jnp.zeros((N, H, dk), jnp.float32),
            jnp.zeros((N, H), jnp.float32))
    _, hs = lax.scan(step, init, xs)
    return jnp.moveaxis(hs, 0, 2).reshape(N, H, S, dv)


def mlstm_mixer(q, k, v, o_pre, gates, i_bias, f_bias, out_norm):
    B, S, _ = q.shape

    def heads(a, d):
        return a.reshape(B, S, ML_HEADS, d).transpose(0, 2, 1, 3).astype(jnp.float32)

    qh = heads(q, ML_DK) * (ML_DK ** -0.5)
    kh = heads(k, ML_DK)
    vh = heads(v, ML_DV)
    g = gates.astype(jnp.float32).reshape(B, S, 2, 2, ML_HEADS)
    i_pre = (g[:, :, :, 0, :] + i_bias.astype(jnp.float32)).transpose(2, 0, 3, 1)
    logf = jax.nn.log_sigmoid(g[:, :, :, 1, :] + f_bias.astype(jnp.float32)).transpose(2, 0, 3, 1)
    qd = jnp.stack([qh, jnp.flip(qh, 2)]).reshape(2 * B, ML_HEADS, S, ML_DK)
    kd = jnp.stack([kh, jnp.flip(kh, 2)]).reshape(2 * B, ML_HEADS, S, ML_DK)
    vd = jnp.stack([vh, jnp.flip(vh, 2)]).reshape(2 * B, ML_HEADS, S, ML_DV)
    i_d = jnp.stack([i_pre[0], jnp.flip(i_pre[1], -1)]).reshape(2 * B, ML_HEADS, S)
    f_d = jnp.stack([logf[0], jnp.flip(logf[1], -1)]).reshape(2 * B, ML_HEADS, S)
    hd = mlstm_chunkwise(qd, kd, vd, i_d, f_d).reshape(2, B, ML_HEADS, S, ML_DV)
    h = hd[0] + jnp.flip(hd[1], 2)
    h = rms_norm(h, out_norm[:, None, :])
    h = h.transpose(0, 2, 1, 3).reshape(B, S, ML_W).astype(q.dtype)
    return jax.nn.sigmoid(o_pre) * h


def mla_blocked_attention(q_nope, q_rope, k_nope, k_rope, v):
    B, H, S, dn = q_nope.shape
    dr = q_rope.shape[-1]
    nb = S // MLA_Q_BLOCK
    scale = (dn + dr) ** -0.5
    qn = q_nope.reshape(B, H, nb, MLA_Q_BLOCK, dn).transpose(2, 0, 1, 3, 4)
    qr = q_rope.reshape(B, H, nb, MLA_Q_BLOCK, dr).transpose(2, 0, 1, 3, 4)

    def block(args):
        qnb, qrb = args
        s = (jnp.einsum('bhqd,bhkd->bhqk', qnb, k_nope)
             + jnp.einsum('bhqd,bkd->bhqk', qrb, k_rope)).astype(jnp.float32) * scale
        p = jax.nn.softmax(s, axis=-1)
        return jnp.einsum('bhqk,bhkd->bhqd', p.astype(v.dtype), v)

    o = lax.map(block, (qn, qr))
    return o.transpose(1, 2, 0, 3, 4).reshape(B, H, S, v.shape[-1])


def mla_mixer(c_q, c_kv_rope, q_lat_norm, w_q_b, kv_lat_norm, w_kv_b, q_head_norm, k_head_norm, cos, sin):
    B, S, _ = c_q.shape
    c_q = rms_norm(c_q, q_lat_norm)
    c_kv, k_rope = jnp.split(c_kv_rope, [MLA_KV_RANK], axis=-1)
    c_kv = rms_norm(c_kv, kv_lat_norm)
    q = jnp.einsum('bsr,re->bse', c_q, w_q_b).reshape(B, S, MLA_HEADS, MLA_NOPE + MLA_ROPE).transpose(0, 2, 1, 3)
    kv = jnp.einsum('bsr,re->bse', c_kv, w_kv_b).reshape(B, S, MLA_HEADS, MLA_NOPE + MLA_V).transpose(0, 2, 1, 3)
    q_nope, q_rope = jnp.split(q, [MLA_NOPE], axis=-1)
    k_nope, v = jnp.split(kv, [MLA_NOPE], axis=-1)
    qg_n, qg_r = jnp.split(q_head_norm, [MLA_NOPE])
    kg_n, kg_r = jnp.split(k_head_norm, [MLA_NOPE])
    q_nope = rms_norm(q_nope, qg_n)
    k_nope = rms_norm(k_nope, kg_n)
    q_rope = apply_rope(rms_norm(q_rope, qg_r), cos, sin)
    k_rope = apply_rope(rms_norm(k_rope, kg_r), cos, sin)
    o = mla_blocked_attention(q_nope, q_rope, k_nope, k_rope, v)
    return o.transpose(0, 2, 1, 3).reshape(B, S, MLA_W)


def banded_attention(q, k, v, half):
    N, L, D = q.shape
    W = half
    nb = -(-L // W)
    Lp = nb * W
    qb = jnp.pad(q, ((0, 0), (0, Lp - L), (0, 0))).reshape(N, nb, W, D)
    kp = jnp.pad(k, ((0, 0), (W, Lp - L + W), (0, 0))).reshape(N, nb + 2, W, D)
    vp = jnp.pad(v, ((0, 0), (W, Lp - L + W), (0, 0))).reshape(N, nb + 2, W, D)
    kb = jnp.concatenate([kp[:, :-2], kp[:, 1:-1], kp[:, 2:]], axis=2)
    vb = jnp.concatenate([vp[:, :-2], vp[:, 1:-1], vp[:, 2:]], axis=2)
    qpos = jnp.arange(Lp).reshape(nb, W)
    kpos = jnp.arange(nb)[:, None] * W - W + jnp.arange(3 * W)[None, :]
    rel = kpos[:, None, :] - qpos[:, :, None]
    mask = (jnp.abs(rel) <= W) & (kpos[:, None, :] >= 0) & (kpos[:, None, :] < L)
    s = jnp.einsum('nbqd,nbkd->nbqk', qb, kb).astype(jnp.float32) * (D ** -0.5)
    s = jnp.where(mask, s, NEG_INF)
    m = jnp.max(s, axis=-1, keepdims=True)
    p = jnp.exp(s - m)
    denom = jnp.sum(p, axis=-1, keepdims=True)
    o = jnp.einsum('nbqk,nbkd->nbqd', (p / denom).astype(v.dtype), vb)
    lse = (m + jnp.log(denom))[..., 0]
    return o.reshape(N, Lp, D)[:, :L], lse.reshape(N, Lp)[:, :L]


def dilated_branch(q, k, v, window, dil):
    B, H, S, D = q.shape
    L = S // dil

    def strided(a):
        return a.reshape(B, H, L, dil, D).transpose(0, 1, 3, 2, 4).reshape(B * H * dil, L, D)

    o, lse = banded_attention(strided(q), strided(k), strided(v), window // (2 * dil))
    o = o.reshape(B, H, dil, L, D).transpose(0, 1, 3, 2, 4).reshape(B, H, S, D)
    lse = lse.reshape(B, H, dil, L).transpose(0, 1, 3, 2).reshape(B, H, S)
    return o, lse


def dilated_mixer(q, k, v, q_norm, k_norm, cos, sin):
    B, S, _ = q.shape

    def heads(a):
        return a.reshape(B, S, DIL_HEADS, DIL_DH).transpose(0, 2, 1, 3)

    def partial_rope(a):
        return jnp.concatenate([apply_rope(a[..., :DIL_ROT], cos, sin), a[..., DIL_ROT:]], axis=-1)

    qh = partial_rope(rms_norm(heads(q), q_norm))
    kh = partial_rope(rms_norm(heads(k), k_norm))
    vh = heads(v)
    outs = []
    lses = []
    for window, dil in DIL_PAIRS:
        o, lse = dilated_branch(qh, kh, vh, window, dil)
        outs.append(o)
        lses.append(lse)
    wts = jax.nn.softmax(jnp.stack(lses), axis=0)
    o = jnp.einsum('pbhs,pbhsd->bhsd', wts, jnp.stack(outs).astype(jnp.float32)).astype(v.dtype)
    return o.transpose(0, 2, 1, 3).reshape(B, S, DIL_W)


def hybrid_layer(x, norm_mix, w_in, ml_i_bias, ml_f_bias, ml_out_norm, mla_q_norm, mla_w_q_b,
                 mla_kv_norm, mla_w_kv_b, mla_q_head_norm, mla_k_head_norm, dil_q_norm, dil_k_norm,
                 w_out, norm_ff, w_ff1, w_ff2, cos_b, sin_b, cos_c, sin_c):
    h = rms_norm(x, norm_mix)
    z = jnp.einsum('bsd,de->bse', h, w_in)
    (ml_q, ml_k, ml_v, ml_o, ml_g, mla_cq, mla_ckv, dil_q, dil_k, dil_v) = jnp.split(
        z, np.cumsum(IN_SPLITS)[:-1].tolist(), axis=-1)
    y_a = mlstm_mixer(ml_q, ml_k, ml_v, ml_o, ml_g, ml_i_bias, ml_f_bias, ml_out_norm)
    y_b = mla_mixer(mla_cq, mla_ckv, mla_q_norm, mla_w_q_b, mla_kv_norm, mla_w_kv_b,
                    mla_q_head_norm, mla_k_head_norm, cos_b, sin_b)
    y_c = dilated_mixer(dil_q, dil_k, dil_v, dil_q_norm, dil_k_norm, cos_c, sin_c)
    y = jnp.concatenate([y_a, y_b, y_c], axis=-1)
    x = x + jnp.einsum('bse,ed->bsd', y, w_out)
    h = rms_norm(x, norm_ff)
    u = jax.nn.relu(jnp.einsum('bsd,df->bsf', h, w_ff1))
    return x + jnp.einsum('bsf,fd->bsd', u * u, w_ff2)


def setup_inputs(seed: int = 0) -> dict:
    key = jax.random.key(seed)
    ks = jax.random.split(key, 19)

    def dense(k, shape, fan_in):
        return jax.random.normal(k, shape, jnp.float32) * (fan_in ** -0.5)

    def gain(k, shape):
        return 1.0 + 0.02 * jax.random.normal(k, shape, jnp.float32)

    x = jax.random.normal(ks[0], (BATCH, SEQ, D_MODEL), jnp.float32)
    ml_i_bias = 0.1 * jax.random.normal(ks[3], (DEPTH, 2, ML_HEADS), jnp.float32)
    ml_f_bias = (jnp.linspace(3.0, 6.0, ML_HEADS, dtype=jnp.float32)[None, None, :]
                 + 0.1 * jax.random.normal(ks[4], (DEPTH, 2, ML_HEADS), jnp.float32))
    return {
        'x': x,
        'norm_mix': gain(ks[1], (DEPTH, D_MODEL)),
        'w_in': dense(ks[2], (DEPTH, D_MODEL, D_IN), D_MODEL),
        'ml_i_bias': ml_i_bias,
        'ml_f_bias': ml_f_bias,
        'ml_out_norm': gain(ks[5], (DEPTH, ML_HEADS, ML_DV)),
        'mla_q_norm': gain(ks[6], (DEPTH, MLA_Q_RANK)),
        'mla_w_q_b': dense(ks[7], (DEPTH, MLA_Q_RANK, MLA_HEADS * (MLA_NOPE + MLA_ROPE)), MLA_Q_RANK),
        'mla_kv_norm': gain(ks[8], (DEPTH, MLA_KV_RANK)),
        'mla_w_kv_b': dense(ks[9], (DEPTH, MLA_KV_RANK, MLA_HEADS * (MLA_NOPE + MLA_V)), MLA_KV_RANK),
        'mla_q_head_norm': gain(ks[10], (DEPTH, MLA_NOPE + MLA_ROPE)),
        'mla_k_head_norm': gain(ks[11], (DEPTH, MLA_NOPE + MLA_ROPE)),
        'dil_q_norm': gain(ks[12], (DEPTH, DIL_DH)),
        'dil_k_norm': gain(ks[13], (DEPTH, DIL_DH)),
        'w_out': dense(ks[14], (DEPTH, D_MIX, D_MODEL), D_MIX),
        'norm_ff': gain(ks[15], (DEPTH, D_MODEL)),
        'w_ff1': dense(ks[16], (DEPTH, D_MODEL, D_FF), D_MODEL),
        'w_ff2': dense(ks[17], (DEPTH, D_FF, D_MODEL), D_FF),
    }


def reference(x, norm_mix, w_in, ml_i_bias, ml_f_bias, ml_out_norm, mla_q_norm, mla_w_q_b,
              mla_kv_norm, mla_w_kv_b, mla_q_head_norm, mla_k_head_norm, dil_q_norm, dil_k_norm,
              w_out, norm_ff, w_ff1, w_ff2):
    seq = x.shape[1]
    cos_b, sin_b = rope_tables(seq, MLA_ROPE)
    cos_c, sin_c = rope_tables(seq, DIL_ROT)
    for l in range(DEPTH):
        x = hybrid_layer(x, norm_mix[l], w_in[l], ml_i_bias[l], ml_f_bias[l], ml_out_norm[l],
                         mla_q_norm[l], mla_w_q_b[l], mla_kv_norm[l], mla_w_kv_b[l],
                         mla_q_head_norm[l], mla_k_head_norm[l], dil_q_norm[l], dil_k_norm[l],
                         w_out[l], norm_ff[l], w_ff1[l], w_ff2[l], cos_b, sin_b, cos_c, sin_c)
    return x
```

```python
import numpy as np
import concourse.bass as bass
import concourse.mybir as mybir
from concourse.bass_utils import run_bass_kernel_spmd

F32 = mybir.dt.float32
BF16 = mybir.dt.bfloat16
ALU = mybir.AluOpType
AF = mybir.ActivationFunctionType
AX = mybir.AxisListType

NCORES = 8
D_MODEL = 2048
SEQ = 16384
BATCH = 2
DEPTH = 2
EPS = 1e-6


class Tok:
    __slots__ = ("name", "last_w", "readers")

    def __init__(self, name=""):
        self.name = name
        self.last_w = None
        self.readers = []


class _Op:
    __slots__ = ("eng", "fn", "reads", "writes", "dma", "chan", "chan_idx", "waits", "signal", "sig_no", "inc", "barrier", "slot")

    def __init__(self, eng, fn, reads, writes, dma, chan, inc=16):
        self.eng, self.fn, self.reads, self.writes, self.dma, self.chan = eng, fn, reads, writes, dma, chan
        self.inc = inc
        self.barrier = False
        self.slot = -1
        self.chan_idx = 0
        self.waits = []
        self.signal = False
        self.sig_no = 0


class Prog:
    EPOCH = 30000
    COMPUTE = ("pe", "act", "dve", "pool")

    def __init__(self, nc):
        self.nc = nc
        self.ops = []
        self.chans = {}
        self.live = []

    def tok(self, name=""):
        t = Tok(name)
        self.live.append(t)
        return t

    def toks(self, n, name=""):
        return [self.tok(f"{name}{i}") for i in range(n)]

    def barrier(self):
        g = Tok("bar")
        o = _Op("sp", lambda e: e.nop(), [], list(self.live) + [g], False, None)
        o.barrier = True
        self.ops.append(o)
        for en in ("pe", "act", "dve", "pool"):
            self.ops.append(_Op(en, lambda e: e.nop(), [g], [], False, None))
        self.live = []

    def op(self, eng, fn, reads=(), writes=()):
        self.ops.append(_Op(eng, fn, list(reads), list(writes), False, None))

    def dma(self, eng, out, in_, reads=(), writes=(), chan=None):
        if chan is None:
            chan = writes[0] if writes else reads[0]
        o = _Op(eng, lambda e: e.dma_start(out=out, in_=in_), list(reads), list(writes), True, chan)
        self.ops.append(o)

    def collective(self, kind, src, dst, groups, reads=(), writes=()):
        tk = self.tok("cc")
        o = _Op("pool", lambda e: e.collective_compute(kind, ALU.bypass, replica_groups=groups, ins=[src], outs=[dst]),
                list(reads), list(writes) + [tk], True, tk, inc=1)
        self.ops.append(o)

    def finalize(self):
        ops = self.ops
        engs = ("pe", "act", "dve", "pool", "sp")
        waited = {e: {} for e in engs}
        active, free_slots, slot_total = {}, [], []
        for i, op in enumerate(ops):
            strong, weak = set(), set()
            for t in op.reads:
                if t.last_w is not None:
                    strong.add(t.last_w)
            for t in op.writes:
                if t.last_w is not None:
                    strong.add(t.last_w)
                weak.update(t.readers)
            need = {}
            for j in strong | weak:
                if j == i:
                    continue
                src = ops[j]
                if src.dma:
                    key, val = ("c", src.slot), src.chan_idx
                else:
                    if src.eng == op.eng and not op.dma:
                        if op.eng == "pe":
                            continue
                        if j not in strong:
                            continue
                    key, val = ("e", src.eng), j
                if val > need.get(key, (-1, None))[0]:
                    need[key] = (val, src)
            w = waited[op.eng]
            for key, (val, src) in need.items():
                if w.get(key, -1) >= val:
                    continue
                w[key] = val
                op.waits.append((key, val, src))
                if not src.dma:
                    src.signal = True
            if op.dma:
                c = id(op.chan)
                if c not in active:
                    active[c] = free_slots.pop() if free_slots else len(slot_total)
                    if active[c] == len(slot_total):
                        slot_total.append(0)
                op.slot = active[c]
                slot_total[op.slot] += op.inc
                op.chan_idx = slot_total[op.slot]
            if op.barrier:
                free_slots.extend(active.values())
                active.clear()
            for t in op.reads:
                t.readers.append(i)
            for t in op.writes:
                t.last_w = i
                t.readers = []
        cnt = {e: 0 for e in engs}
        for op in ops:
            if op.signal:
                cnt[op.eng] += 1
                op.sig_no = cnt[op.eng]
        nc = self.nc
        self.esems = {e: [nc.alloc_semaphore(name=f"s_{e}_{k}") for k in range((cnt[e] + self.EPOCH - 1) // self.EPOCH)]
                      for e in engs}
        self.csems = {}
        for c, n in enumerate(slot_total):
            assert n < 60000, f"too many DMAs on one channel semaphore: {n}"
            self.csems[c] = nc.alloc_semaphore(name=f"c_{c}")
        self.stats = dict(n_ops=len(ops), signals=dict(cnt), chans=len(slot_total), max_chan=max(slot_total or [0]),
                          per_eng={e: sum(1 for o in ops if o.eng == e) for e in engs})

    def emit(self, final_waits=()):
        nc = self.nc
        ops = self.ops
        E = self.EPOCH

        def run(engname, eng):
            for op in ops:
                if op.eng != engname:
                    continue
                for key, val, src in op.waits:
                    if key[0] == "c":
                        eng.wait_ge(self.csems[key[1]], val)
                    else:
                        s = src.sig_no - 1
                        eng.wait_ge(self.esems[src.eng][s // E], s % E + 1)
                ins = op.fn(eng)
                if op.dma:
                    ins.then_inc(self.csems[op.slot], op.inc)
                elif op.signal:
                    s = op.sig_no - 1
                    ins.then_inc(self.esems[op.eng][s // E], 1)

        with nc.Block() as block:
            @block.tensor
            def _(e):
                run("pe", e)

            @block.scalar
            def _(e):
                run("act", e)

            @block.vector
            def _(e):
                run("dve", e)

            @block.gpsimd
            def _(e):
                run("pool", e)

            @block.sync
            def _(e):
                run("sp", e)


class Ctx:
    def __init__(self, nc, P, bind=None, tag=""):
        import contextlib
        self.nc, self.P, self.bind, self.tag = nc, P, dict(bind or {}), tag
        self.stack = contextlib.ExitStack()

    def din(self, name, shape, dtype):
        if name in self.bind:
            ap = self.bind[name]
            assert tuple(ap.shape) == tuple(shape), (name, ap.shape, shape)
            return ap
        return self.nc.dram_tensor(name, list(shape), dtype, kind="ExternalInput").ap()

    def dout(self, name, shape, dtype):
        if name in self.bind:
            ap = self.bind[name]
            assert tuple(ap.shape) == tuple(shape), (name, ap.shape, shape)
            return ap
        return self.nc.dram_tensor(name, list(shape), dtype, kind="ExternalOutput").ap()

    def sb(self, name, shape, dt):
        return self.stack.enter_context(self.nc.sbuf_tensor(f"s_{self.tag}{name}", list(shape), dt))

    def ps(self, name, shape, dt):
        return self.stack.enter_context(self.nc.psum_tensor(f"p_{self.tag}{name}", list(shape), dt))

    def end(self):
        self.P.barrier()
        self.stack.close()


def _standalone(emit, *args, **kw):
    nc = bass.Bass("TRN2", target_bir_lowering=False)
    P = Prog(nc)
    ctx = Ctx(nc, P)
    emit(ctx, *args, **kw)
    P.finalize()
    P.emit()
    return nc, P


class Ring:
    def __init__(self, items):
        self.items = items
        self.i = 0

    def next(self):
        it = self.items[self.i % len(self.items)]
        self.i += 1
        return it


def _final_wait(P, toks):
    nc = P.nc
    P.op("sp", lambda e: e.nop(), reads=list(toks))


def emit_cast(ctx, NI):
    nc, P = ctx.nc, ctx.P
    src = ctx.din("src", [128, NI, 4096], F32)
    dst = ctx.dout("dst", [128, NI, 4096], BF16)
    CH = 2048
    nb = 3
    ins = [(ctx.sb(f"ci{i}", [128, CH], F32), P.tok()) for i in range(nb)]
    outs = [(ctx.sb(f"co{i}", [128, CH], BF16), P.tok()) for i in range(nb)]
    engs = ["dve", "pool", "act"]
    c = 0
    for i in range(NI):
        for hh in range(4096 // CH):
            it, itok = ins[c % nb]
            ot, otok = outs[c % nb]
            P.dma("sp", it[:], src[:, i, hh * CH:(hh + 1) * CH], writes=[itok])
            en = engs[c % 3]
            if en == "act":
                P.op("act", lambda e, ot=ot, it=it: e.activation(out=ot[:], in_=it[:], func=AF.Copy), reads=[itok], writes=[otok])
            else:
                P.op(en, lambda e, ot=ot, it=it: e.tensor_copy(out=ot[:], in_=it[:]), reads=[itok], writes=[otok])
            P.dma("pool", dst[:, i, hh * CH:(hh + 1) * CH], ot[:], reads=[otok], writes=[], chan=otok)
            c += 1
    ctx.end()


def emit_C(ctx, T, TS=512):
    nc, P = ctx.nc, ctx.P
    KC = 16
    xT_d = ctx.din("xT", [128, KC, T], F32)
    yT_d = ctx.din("yT", [128, KC, T], BF16)
    wo_d = ctx.din("wout_b", [16, 128, KC * 128], BF16)
    w1_d = ctx.din("w1_b", [64, 128, KC * 128], BF16)
    w2_d = ctx.din("w2_b", [64, 128, KC * 128], BF16)
    g_d = ctx.din("gff", [128, KC], F32)
    xo_d = ctx.dout("xoT", [128, KC, T], F32)
    NS = T // TS

    def sb(name, shape, dt):
        return ctx.sb(name, shape, dt)

    xs = [(sb(f"xs{i}", [128, KC, TS], F32), P.toks(KC)) for i in range(2)]
    ys = [(sb(f"ys{i}", [128, KC, TS], BF16), P.tok()) for i in range(2)]
    hT = (sb("hT", [128, KC, TS], BF16), P.toks(KC))
    u2 = [(sb(f"u2{i}", [128, 16, TS], BF16), P.toks(16)) for i in range(2)]
    NW = 6
    wb = Ring([(sb(f"wb{i}", [128, KC * 128], BF16), P.tok()) for i in range(NW)])
    sqr = Ring([(sb(f"sq{i}", [128, TS], BF16), P.tok()) for i in range(3)])
    rl = Ring([(sb(f"rl{i}", [128, TS], F32), P.tok()) for i in range(3)])
    rstd = (sb("rstd", [128, TS], F32), P.tok())
    ones = (sb("ones", [128, 128], BF16), P.tok())
    gff = (sb("gffs", [128, KC], F32), P.tok())
    pss = Ring([(ctx.ps(f"ps{i}", [128, TS], F32), P.tok()) for i in range(6)])
    ps_ss = (ctx.ps("ps_ss", [128, TS], F32), P.tok())
    out_tok = P.tok("xo")

    P.op("pool", lambda e: e.memset(ones[0][:], 1.0 / D_MODEL), writes=[ones[1]])
    P.dma("sp", gff[0][:], g_d, writes=[gff[1]])

    wlist = []
    for s in range(NS):
        wlist += [wo_d[cc] for cc in range(16)]
        seq = ["f1q0", "f1q1", "f2q0", "f1q2", "f2q1", "f1q3", "f2q2", "f2q3"]
        for st in seq:
            q = int(st[3])
            if st.startswith("f1"):
                wlist += [w1_d[q * 16 + j] for j in range(16)]
            else:
                wlist += [w2_d[q * 16 + cc] for cc in range(16)]
    wstate = {"issued": 0, "used": 0, "cur": {}}

    def w_prefetch(upto):
        while wstate["issued"] < min(upto, len(wlist)):
            i = wstate["issued"]
            t, tk = wb.next()
            P.dma("sp", t[:], wlist[i], writes=[tk])
            wstate["cur"][i] = (t, tk)
            wstate["issued"] += 1

    def w_get():
        i = wstate["used"]
        w_prefetch(i + NW - 1)
        wstate["used"] += 1
        w_prefetch(i + NW)
        return wstate["cur"].pop(i)

    def load_xy(s):
        xt, xtk = xs[s % 2]
        yt, ytk = ys[s % 2]
        for h in range(2):
            P.dma("pool", xt[:, h * 8:(h + 1) * 8, :], xT_d[:, h * 8:(h + 1) * 8, s * TS:(s + 1) * TS],
                  writes=xtk[h * 8:(h + 1) * 8], chan=xtk[h * 8])
        P.dma("pool", yt[:], yT_d[:, :, s * TS:(s + 1) * TS], writes=[ytk])

    load_xy(0)
    for s in range(NS):
        xt, xtk = xs[s % 2]
        yt, ytk = ys[s % 2]
        if s + 1 < NS:
            load_xy(s + 1)
        for cc in range(16):
            wt, wtk = w_get()
            pt, ptk = pss.next()
            for kc in range(KC):
                P.op("pe", lambda e, pt=pt, wt=wt, yt=yt, kc=kc: e.matmul(
                    pt[:], wt[:, kc * 128:(kc + 1) * 128], yt[:, kc, :], start=(kc == 0), stop=(kc == KC - 1)),
                    reads=[wtk, ytk], writes=[ptk])
            P.op("dve", lambda e, xt=xt, pt=pt, cc=cc: e.tensor_tensor(
                out=xt[:, cc, :], in0=xt[:, cc, :], in1=pt[:], op=ALU.add), reads=[ptk, xtk[cc]], writes=[xtk[cc]])
            st_, stk = sqr.next()
            P.op("act", lambda e, st_=st_, xt=xt, cc=cc: e.activation(out=st_[:], in_=xt[:, cc, :], func=AF.Square),
                 reads=[xtk[cc]], writes=[stk])
            P.op("pe", lambda e, st_=st_, cc=cc: e.matmul(ps_ss[0][:], ones[0][:], st_[:], start=(cc == 0), stop=(cc == 15)),
                 reads=[stk, ones[1]], writes=[ps_ss[1]])
        P.op("dve", lambda e: e.tensor_scalar_add(out=rstd[0][:], in0=ps_ss[0][:], scalar1=EPS),
             reads=[ps_ss[1]], writes=[rstd[1]])
        P.op("act", lambda e: e.activation(out=rstd[0][:], in_=rstd[0][:], func=AF.Sqrt),
             reads=[rstd[1]], writes=[rstd[1]])
        P.op("dve", lambda e: e.reciprocal(out=rstd[0][:], in_=rstd[0][:]),
             reads=[rstd[1]], writes=[rstd[1]])
        for kc in range(KC):
            P.op("dve", lambda e, xt=xt, kc=kc: e.scalar_tensor_tensor(
                out=hT[0][:, kc, :], in0=xt[:, kc, :], scalar=gff[0][:, kc:kc + 1], in1=rstd[0][:],
                op0=ALU.mult, op1=ALU.mult), reads=[xtk[kc], gff[1], rstd[1]], writes=[hT[1][kc]])

        def f1(q):
            ut, utk = u2[q % 2]
            for j in range(16):
                wt, wtk = w_get()
                pt, ptk = pss.next()
                for kc in range(KC):
                    P.op("pe", lambda e, pt=pt, wt=wt, kc=kc: e.matmul(
                        pt[:], wt[:, kc * 128:(kc + 1) * 128], hT[0][:, kc, :], start=(kc == 0), stop=(kc == KC - 1)),
                        reads=[wtk, hT[1][kc]], writes=[ptk])
                rt, rtk = rl.next()
                P.op("act", lambda e, rt=rt, pt=pt: e.activation(out=rt[:], in_=pt[:], func=AF.Relu),
                     reads=[ptk], writes=[rtk])
                P.op("pool", lambda e, ut=ut, rt=rt, j=j: e.tensor_tensor(out=ut[:, j, :], in0=rt[:], in1=rt[:], op=ALU.mult),
                     reads=[rtk], writes=[utk[j]])

        def f2(q):
            ut, utk = u2[q % 2]
            for cc in range(16):
                wt, wtk = w_get()
                pt, ptk = pss.next()
                for kc in range(16):
                    P.op("pe", lambda e, pt=pt, wt=wt, ut=ut, kc=kc: e.matmul(
                        pt[:], wt[:, kc * 128:(kc + 1) * 128], ut[:, kc, :], start=(kc == 0), stop=(kc == 15)),
                        reads=[wtk, utk[kc]], writes=[ptk])
                P.op("dve", lambda e, xt=xt, pt=pt, cc=cc: e.tensor_tensor(
                    out=xt[:, cc, :], in0=xt[:, cc, :], in1=pt[:], op=ALU.add), reads=[ptk, xtk[cc]], writes=[xtk[cc]])
                if q == 3:
                    P.dma("pool", xo_d[:, cc, s * TS:(s + 1) * TS], xt[:, cc, :], reads=[xtk[cc]], writes=[out_tok],
                          chan=xtk[cc])

        f1(0); f1(1); f2(0); f1(2); f2(1); f1(3); f2(2); f2(3)
    ctx.end()


C_MLQ, C_MLK, C_MLV, C_MLO, C_MLG, C_CQ, C_CKV, C_DQ, C_DK, C_DV = 0, 256, 512, 1024, 1536, 1552, 1936, 2128, 2896, 3664
D_IN = 4432
TOK_GROUPS = [("mlk", 256, 256), ("mlv", 512, 512), ("mlo", 1024, 512), ("gq", 1536, 400), ("ckv", 1936, 192),
              ("dq0", 2128, 384), ("dq1", 2512, 384), ("dk0", 2896, 384), ("dk1", 3280, 384),
              ("dv0", 3664, 384), ("dv1", 4048, 384)]
FM_CHUNKS = [("mlq0", 0), ("mlq1", 128), ("mlk0", 256), ("mlk1", 384), ("cq0", 1552), ("cq1", 1680), ("cq2", 1808),
             ("ckv", 1936)]
MLA_SCALE = 192 ** -0.5
DIL_SCALE = 128 ** -0.5


def emit_A(ctx, T, TS=512):
    nc, P = ctx.nc, ctx.P
    KC = 16
    NS = T // TS
    dt_ = nc.dram_tensor
    xT_d = ctx.din("xT", [128, KC, T], F32)
    win_d = ctx.din("win", [128, KC, D_IN], BF16)
    gmix_d = ctx.din("gmix", [128, KC], F32)
    wq_d = ctx.din("wq", [128, 3, 1152], F32)
    wkv_d = ctx.din("wkv", [128, 1536], F32)
    qlat_d = ctx.din("qlat_g", [128, 3], F32)
    kvlat_d = ctx.din("kvlat_g", [128, 1], F32)
    gains_d = ctx.din("gains", [128, 704], F32)
    rope_d = ctx.din("rope", [T, 96], F32)
    ident_d = ctx.din("ident", [128, 128], BF16)
    o_qTml = ctx.dout("o_qTml", [4, 64, T], BF16)
    o_kTml = ctx.dout("o_kTml", [4, 64, T], BF16)
    NJc = T // 128
    o_kb = ctx.dout("o_kb", [4, 128, NJc, 64], BF16)
    o_vb = ctx.dout("o_vb", [4, 128, NJc, 128], BF16)
    o_osb = ctx.dout("o_osb", [4, 128, NJc, 128], F32)
    o_gb = ctx.dout("o_gb", [128, NJc, 16], F32)
    o_QTn = ctx.dout("o_QTn", [6, 128, T], BF16)
    o_QTr = ctx.dout("o_QTr", [6, 64, T], BF16)
    o_KTn = ctx.dout("o_KTn", [6, 128, T], BF16)
    o_KTr = ctx.dout("o_KTr", [64, T], BF16)
    o_Vb = ctx.dout("o_Vb", [6, 128, T], BF16)
    o_QTd = ctx.dout("o_QTd", [6, 128, T], BF16)
    o_KTd = ctx.dout("o_KTd", [6, 128, T], BF16)
    o_Vdb = ctx.dout("o_Vdb", [6, 128, T], BF16)
    cnt = [0]

    def sb(shape, dt, name=None):
        cnt[0] += 1
        return ctx.sb((name or f"a{cnt[0]}"), shape, dt)

    def ring(n, shape, dt, name):
        return Ring([(sb(shape, dt, f"{name}{i}"), P.tok(f"{name}{i}")) for i in range(n)])

    def psring(n, shape, dt, name):
        return Ring([(ctx.ps(f"{name}{i}", shape, dt), P.tok(f"{name}{i}")) for i in range(n)])

    outs = []

    xs = (sb([128, KC, TS], F32, "xs"), P.toks(KC))
    hT = (sb([128, KC, TS], BF16, "hT"), P.toks(KC))
    wtok = ring(2, [128, KC, 512], BF16, "wtok")
    wfm = ring(3, [128, KC, 128], BF16, "wfm")
    sqr = ring(3, [128, TS], BF16, "sq")
    rstd = (sb([128, TS], F32, "rstd"), P.tok())
    ones = (sb([128, 128], BF16, "ones"), P.tok())
    ident = (sb([128, 128], BF16, "ident"), P.tok())
    gmix = (sb([128, KC], F32, "gmix"), P.tok())
    wq = (sb([128, 3, 1152], BF16, "wq"), P.tok())
    wkv = (sb([128, 1536], BF16, "wkv"), P.tok())
    wtmp = (sb([128, 3, 1152], F32, "wtmp"), P.tok())
    qlat = (sb([128, 3], F32, "qlat"), P.tok())
    kvlat = (sb([128, 1], F32, "kvlat"), P.tok())
    gains = (sb([128, 704], F32, "gains"), P.tok())
    cqT = [(sb([128, TS], BF16, f"cqT{i}"), P.tok()) for i in range(3)]
    ckvT = (sb([128, TS], BF16, "ckvT"), P.tok())
    fmout = ring(3, [128, TS], BF16, "fmout")
    rope_t = ring(4, [128, 96], F32, "rope")
    qfx4 = ring(4, [128, 384], F32, "qfx")
    qbq4 = ring(4, [128, 1152], BF16, "qbq")
    stats = ring(8, [128, 16], F32, "stats")
    sqs = ring(2, [128, 1152], F32, "sqs")
    fx = ring(2, [128, 1152], F32, "fx")
    rtmp = ring(3, [128, 6, 32], F32, "rtmp")
    qb = ring(3, [128, 1152], BF16, "qb")
    ob32 = ring(2, [128, 512], F32, "ob32")
    tst = ring(4, [128, 4, 128], BF16, "tst")
    ps_ss = (ctx.ps("ps_ss", [128, TS], F32), P.tok())
    pg = psring(3, [128, 512], F32, "pg")
    pr = psring(2, [128, 512], F32, "pr")
    ptr = psring(2, [128, 4, 128], BF16, "ptr")

    P.op("pool", lambda e: e.memset(ones[0][:], 1.0 / D_MODEL), writes=[ones[1]])
    P.dma("sp", ident[0][:], ident_d, writes=[ident[1]])
    P.dma("sp", gmix[0][:], gmix_d, writes=[gmix[1]])
    P.dma("sp", qlat[0][:], qlat_d, writes=[qlat[1]])
    P.dma("sp", kvlat[0][:], kvlat_d, writes=[kvlat[1]])
    P.dma("sp", gains[0][:], gains_d, writes=[gains[1]])
    P.dma("sp", wtmp[0][:], wq_d, writes=[wtmp[1]])
    for kc in range(3):
        P.op("dve", lambda e, kc=kc: e.tensor_scalar_mul(out=wq[0][:, kc, :], in0=wtmp[0][:, kc, :], scalar1=qlat[0][:, kc:kc + 1]),
             reads=[wtmp[1], qlat[1]], writes=[wq[1]])
    wt2 = wtmp[0][:].rearrange("p a b -> p (a b)")
    P.dma("sp", wt2[:, 0:1536], wkv_d, writes=[wtmp[1]])
    P.op("dve", lambda e: e.tensor_scalar_mul(out=wkv[0][:], in0=wt2[:, 0:1536], scalar1=kvlat[0][:, 0:1]),
         reads=[wtmp[1], kvlat[1]], writes=[wkv[1]])

    G_QH, G_KH, G_DQ, G_DK = 0, 192, 384, 512

    def out_dma(dst, src, tk):
        P.dma("pool", dst, src, reads=[tk], writes=[], chan=tk)
        if tk not in outs:
            outs.append(tk)

    def rsqrt_inplace(t_ap, tk):
        P.op("act", lambda e: e.activation(out=t_ap, in_=t_ap, func=AF.Ln), reads=[tk], writes=[tk])
        P.op("act", lambda e: e.activation(out=t_ap, in_=t_ap, func=AF.Exp, scale=-0.5), reads=[tk], writes=[tk])

    def head_norm(src, srctk, H, dh, pre, gain_ap, scale, dst, dsttk, st, sttk, c0):
        sq, sqk = sqs.next()
        sqv = sq[:, 0:H * dh].rearrange("p (h d) -> p h d", h=H)
        P.op("act", lambda e: e.activation(out=sqv, in_=src, func=AF.Square, scale=float(dh) ** -0.5),
             reads=[srctk], writes=[sqk])
        ms = st[:, c0:c0 + H]
        P.op("dve", lambda e: e.tensor_reduce(out=ms, in_=sqv, axis=AX.X, op=ALU.add), reads=[sqk], writes=[sttk])
        if pre is not None:
            pap, ptk = pre
            P.op("dve", lambda e: e.tensor_scalar(out=ms, in0=ms, scalar1=pap, scalar2=pap, op0=ALU.mult, op1=ALU.mult),
                 reads=[sttk, ptk], writes=[sttk])
        P.op("dve", lambda e: e.tensor_scalar_add(out=ms, in0=ms, scalar1=EPS), reads=[sttk], writes=[sttk])
        rsqrt_inplace(ms, sttk)
        if pre is not None:
            P.op("dve", lambda e: e.tensor_scalar(out=ms, in0=ms, scalar1=pap, scalar2=float(scale), op0=ALU.mult, op1=ALU.mult),
                 reads=[sttk, ptk], writes=[sttk])
        elif scale != 1.0:
            P.op("dve", lambda e: e.tensor_scalar_mul(out=ms, in0=ms, scalar1=float(scale)), reads=[sttk], writes=[sttk])
        P.op("dve", lambda e: e.tensor_tensor(out=dst, in0=src, in1=ms.unsqueeze(2).to_broadcast([128, H, dh]), op=ALU.mult),
             reads=[srctk, sttk], writes=[dsttk])
        P.op("pool", lambda e: e.tensor_tensor(out=dst, in0=dst, in1=gain_ap.unsqueeze(1).to_broadcast([128, H, dh]), op=ALU.mult),
             reads=[dsttk, gains[1]], writes=[dsttk])

    def rope(x1, x2, H, hd, cos, sin, ropetk, srctk, o1, o2, otk):
        cb = cos.unsqueeze(1).to_broadcast([128, H, hd])
        sn = sin.unsqueeze(1).to_broadcast([128, H, hd])
        ta, tak = rtmp.next()
        tb, tbk = rtmp.next()
        a = ta[:, 0:H, 0:hd]
        b = tb[:, 0:H, 0:hd]
        P.op("dve", lambda e: e.tensor_tensor(out=a, in0=x2, in1=sn, op=ALU.mult), reads=[srctk, ropetk], writes=[tak])
        P.op("pool", lambda e: e.tensor_tensor(out=b, in0=x1, in1=sn, op=ALU.mult), reads=[srctk, ropetk], writes=[tbk])
        tc_, tck = rtmp.next()
        c = tc_[:, 0:H, 0:hd]
        P.op("dve", lambda e: e.tensor_tensor(out=c, in0=x1, in1=cb, op=ALU.mult), reads=[srctk, ropetk], writes=[tck])
        P.op("dve", lambda e: e.tensor_tensor(out=o1, in0=c, in1=a, op=ALU.subtract), reads=[tck, tak], writes=[otk])
        td, tdk = rtmp.next()
        dd = td[:, 0:H, 0:hd]
        P.op("pool", lambda e: e.tensor_tensor(out=dd, in0=x2, in1=cb, op=ALU.mult), reads=[srctk, ropetk], writes=[tdk])
        P.op("pool", lambda e: e.tensor_tensor(out=o2, in0=dd, in1=b, op=ALU.add), reads=[tdk, tbk], writes=[otk])

    def transpose_out(src_fn, srctk, nh, rows, dst_fn):
        h0 = 0
        while h0 < nh:
            n = min(4, nh - h0)
            pt, ptk = ptr.next()
            for i in range(n):
                P.op("pe", lambda e, pt=pt, i=i, h=h0 + i: e.transpose(pt[0:rows, i, :], src_fn(h), ident[0][:]),
                     reads=[srctk, ident[1]], writes=[ptk])
            stt, sttk = tst.next()
            P.op("act", lambda e, pt=pt, stt=stt, n=n: e.activation(out=stt[0:rows, 0:n, :], in_=pt[0:rows, 0:n, :], func=AF.Copy),
                 reads=[ptk], writes=[sttk])
            for i in range(n):
                out_dma(dst_fn(h0 + i), stt[0:rows, i, :], sttk)
            h0 += n

    def load_wtok(g):
        name, c0, w = TOK_GROUPS[g]
        t, tk = wtok.next()
        P.dma("sp", t[:, :, 0:w], win_d[:, :, c0:c0 + w], writes=[tk])
        return t, tk

    def load_wfm(i):
        name, c0 = FM_CHUNKS[i]
        t, tk = wfm.next()
        P.dma("sp", t[:], win_d[:, :, c0:c0 + 128], writes=[tk])
        return t, tk

    for s in range(NS):
        t0 = s * TS
        for h in range(2):
            P.dma("pool", xs[0][:, h * 8:(h + 1) * 8, :], xT_d[:, h * 8:(h + 1) * 8, t0:t0 + TS],
                  writes=xs[1][h * 8:(h + 1) * 8], chan=xs[1][h * 8])
        for kc in range(KC):
            st_, stk = sqr.next()
            P.op("act", lambda e, st_=st_, kc=kc: e.activation(out=st_[:], in_=xs[0][:, kc, :], func=AF.Square),
                 reads=[xs[1][kc]], writes=[stk])
            P.op("pe", lambda e, st_=st_, kc=kc: e.matmul(ps_ss[0][:], ones[0][:], st_[:], start=(kc == 0), stop=(kc == KC - 1)),
                 reads=[stk, ones[1]], writes=[ps_ss[1]])
        P.op("dve", lambda e: e.tensor_scalar_add(out=rstd[0][:], in0=ps_ss[0][:], scalar1=EPS), reads=[ps_ss[1]], writes=[rstd[1]])
        rsqrt_inplace(rstd[0][:], rstd[1])
        for kc in range(KC):
            P.op("dve", lambda e, kc=kc: e.scalar_tensor_tensor(
                out=hT[0][:, kc, :], in0=xs[0][:, kc, :], scalar=gmix[0][:, kc:kc + 1], in1=rstd[0][:],
                op0=ALU.mult, op1=ALU.mult), reads=[xs[1][kc], gmix[1], rstd[1]], writes=[hT[1][kc]])
        for i, (name, c0) in enumerate(FM_CHUNKS):
            wt, wtk = load_wfm(i)
            pt, ptk = pg.next()
            for kc in range(KC):
                P.op("pe", lambda e, pt=pt, wt=wt, kc=kc: e.matmul(pt[:], wt[:, kc, :], hT[0][:, kc, :], start=(kc == 0), stop=(kc == KC - 1)),
                     reads=[wtk, hT[1][kc]], writes=[ptk])
            if name.startswith("ml"):
                ft, ftk = fmout.next()
                P.op("act", lambda e, ft=ft, pt=pt: e.activation(out=ft[:], in_=pt[:], func=AF.Copy), reads=[ptk], writes=[ftk])
                dsto = (o_qTml if name[2] == "q" else o_kTml)
                c2 = int(name[3])
                out_dma(dsto[2 * c2, :, t0:t0 + TS], ft[0:64, :], ftk)
                out_dma(dsto[2 * c2 + 1, :, t0:t0 + TS], ft[64:128, :], ftk)
            else:
                ft, ftk = cqT[int(name[2])] if name.startswith("cq") else ckvT
                P.op("act", lambda e, ft=ft, pt=pt: e.activation(out=ft[:], in_=pt[:], func=AF.Copy), reads=[ptk], writes=[ftk])
        per_tt = [dict() for _ in range(4)]
        for g, (name, c0, w) in enumerate(TOK_GROUPS):
            wt, wtk = load_wtok(g)
            for tt in range(4):
                r0 = t0 + tt * 128
                pt, ptk = pg.next()
                for kc in range(KC):
                    P.op("pe", lambda e, pt=pt, wt=wt, kc=kc, tt=tt, w=w: e.matmul(
                        pt[:, 0:w], hT[0][:, kc, tt * 128:(tt + 1) * 128], wt[:, kc, 0:w], start=(kc == 0), stop=(kc == KC - 1)),
                        reads=[wtk, hT[1][kc]], writes=[ptk])
                if name in ("mlk", "mlv", "dv0", "dv1"):
                    bt, btk = qb.next()
                    P.op("act", lambda e, bt=bt, pt=pt, w=w: e.activation(out=bt[:, 0:w], in_=pt[:, 0:w], func=AF.Copy),
                         reads=[ptk], writes=[btk])
                    jj = r0 // 128
                    if name == "mlk":
                        dst = o_kb[:, :, jj, :].rearrange("h p d -> p h d")
                        srcv = bt[:, 0:256].rearrange("p (h d) -> p h d", h=4)
                    elif name == "mlv":
                        dst = o_vb[:, :, jj, :].rearrange("h p d -> p h d")
                        srcv = bt[:, 0:512].rearrange("p (h d) -> p h d", h=4)
                    else:
                        k = int(name[2])
                        dst = o_Vdb[3 * k:3 * k + 3, :, jj * 128:(jj + 1) * 128].rearrange("h p d -> p h d")
                        srcv = bt[:, 0:384].rearrange("p (h d) -> p h d", h=3)
                    out_dma(dst, srcv, btk)
                elif name == "mlo":
                    ot, otk = ob32.next()
                    P.op("act", lambda e, ot=ot, pt=pt: e.activation(out=ot[:], in_=pt[:], func=AF.Exp, scale=-1.0),
                         reads=[ptk], writes=[otk])
                    P.op("dve", lambda e, ot=ot: e.tensor_scalar_add(out=ot[:], in0=ot[:], scalar1=1.0), reads=[otk], writes=[otk])
                    P.op("dve", lambda e, ot=ot: e.reciprocal(out=ot[:], in_=ot[:]), reads=[otk], writes=[otk])
                    out_dma(o_osb[:, :, r0 // 128, :].rearrange("h p d -> p h d"), ot[:].rearrange("p (h d) -> p h d", h=4), otk)
                elif name == "gq":
                    ot, otk = ob32.next()
                    P.op("dve", lambda e, ot=ot, pt=pt: e.tensor_copy(out=ot[:, 0:16], in_=pt[:, 0:16]), reads=[ptk], writes=[otk])
                    out_dma(o_gb[:, r0 // 128, :], ot[:, 0:16], otk)
                    st, stk = stats.next()
                    sq, sqk = sqs.next()
                    P.op("act", lambda e, sq=sq, pt=pt, st=st: e.activation(out=sq[:, 0:384], in_=pt[:, 16:400], func=AF.Square,
                                                                             scale=384.0 ** -0.5, accum_out=st[:, 15:16]),
                         reads=[ptk], writes=[sqk, stk])
                    P.op("dve", lambda e, st=st: e.tensor_scalar_add(out=st[:, 15:16], in0=st[:, 15:16], scalar1=EPS), reads=[stk], writes=[stk])
                    rsqrt_inplace(st[:, 15:16], stk)
                    pre = (st[:, 15:16], stk)
                    fxt, fxk = qfx4.next()
                    fxv = fxt[:].rearrange("p (h d) -> p h d", h=6)
                    qbt, qbk = qbq4.next()
                    qbv = qbt[:].rearrange("p (h d) -> p h d", h=6)
                    for b in range(3):
                        pq, pqk = pr.next()
                        for kc in range(3):
                            P.op("pe", lambda e, pq=pq, kc=kc, b=b, tt=tt: e.matmul(
                                pq[:, 0:384], cqT[kc][0][:, tt * 128:(tt + 1) * 128], wq[0][:, kc, b * 384:(b + 1) * 384],
                                start=(kc == 0), stop=(kc == 2)), reads=[cqT[kc][1], wq[1]], writes=[pqk])
                        pv = pq[:, 0:384].rearrange("p (h d) -> p h d", h=2)
                        head_norm(pv[:, :, 0:128], pqk, 2, 128, pre, gains[0][:, G_QH:G_QH + 128], MLA_SCALE,
                                  qbv[:, 2 * b:2 * b + 2, 0:128], qbk, st, stk, 2 * b)
                        head_norm(pv[:, :, 128:192], pqk, 2, 64, pre, gains[0][:, G_QH + 128:G_QH + 192], MLA_SCALE,
                                  fxv[:, 2 * b:2 * b + 2, :], fxk, st, stk, 6 + 2 * b)
                    per_tt[tt]["q"] = (fxv, fxk, qbt, qbv, qbk)
                elif name == "ckv":
                    rt, rtk = rope_t.next()
                    P.dma("sp", rt[:], rope_d[r0:r0 + 128, :], writes=[rtk])
                    per_tt[tt]["rope"] = (rt, rtk)
                    fxv, fxk, qbt, qbv, qbk = per_tt[tt]["q"]
                    rope(fxv[:, :, 0:32], fxv[:, :, 32:64], 6, 32, rt[:, 0:32], rt[:, 32:64], rtk, fxk,
                         qbv[:, :, 128:160], qbv[:, :, 160:192], qbk)
                    transpose_out(lambda h, qbv=qbv: qbv[:, h, 0:128], qbk, 6, 128, lambda h: o_QTn[h, :, r0:r0 + 128])
                    transpose_out(lambda h, qbv=qbv: qbv[:, h, 128:192], qbk, 6, 64, lambda h: o_QTr[h, :, r0:r0 + 128])
                    st, stk = stats.next()
                    sq, sqk = sqs.next()
                    P.op("act", lambda e, sq=sq, pt=pt, st=st: e.activation(out=sq[:, 0:128], in_=pt[:, 0:128], func=AF.Square,
                                                                             scale=128.0 ** -0.5, accum_out=st[:, 15:16]),
                         reads=[ptk], writes=[sqk, stk])
                    P.op("dve", lambda e, st=st: e.tensor_scalar_add(out=st[:, 15:16], in0=st[:, 15:16], scalar1=EPS), reads=[stk], writes=[stk])
                    rsqrt_inplace(st[:, 15:16], stk)
                    pre = (st[:, 15:16], stk)
                    fxt, fxk = fx.next()
                    kbt, kbk = qb.next()
                    kr = fxt[:, 0:64].rearrange("p (h d) -> p h d", h=1)
                    head_norm(pt[:, 128:192].rearrange("p (h d) -> p h d", h=1), ptk, 1, 64, None, gains[0][:, G_KH + 128:G_KH + 192], 1.0,
                              kr, fxk, st, stk, 12)
                    kro = kbt[:, 768:832].rearrange("p (h d) -> p h d", h=1)
                    rope(kr[:, :, 0:32], kr[:, :, 32:64], 1, 32, rt[:, 0:32], rt[:, 32:64], rtk, fxk, kro[:, :, 0:32], kro[:, :, 32:64], kbk)
                    kbv = kbt[:, 0:768].rearrange("p (h d) -> p h d", h=6)
                    vt, vtk = qb.next()
                    vv = vt[:, 0:768].rearrange("p (h d) -> p h d", h=6)
                    for b in range(3):
                        pk, pkk = pr.next()
                        P.op("pe", lambda e, pk=pk, b=b, tt=tt: e.matmul(pk[:], ckvT[0][:, tt * 128:(tt + 1) * 128], wkv[0][:, b * 512:(b + 1) * 512],
                                                                           start=True, stop=True), reads=[ckvT[1], wkv[1]], writes=[pkk])
                        pv = pk[:].rearrange("p (h d) -> p h d", h=2)
                        head_norm(pv[:, :, 0:128], pkk, 2, 128, pre, gains[0][:, G_KH:G_KH + 128], 1.0,
                                  kbv[:, 2 * b:2 * b + 2, :], kbk, st, stk, 2 * b)
                        P.op("dve", lambda e, pv=pv, b=b, vv=vv, st=st: e.tensor_scalar_mul(out=vv[:, 2 * b:2 * b + 2, :], in0=pv[:, :, 128:256],
                                                                                            scalar1=st[:, 15:16]), reads=[pkk, stk], writes=[vtk])
                    out_dma(o_Vb[:, :, r0:r0 + 128].rearrange("h p d -> p h d"), vt[:, 0:768].rearrange("p (h d) -> p h d", h=6), vtk)
                    transpose_out(lambda h, kbv=kbv: kbv[:, h, :], kbk, 6, 128, lambda h: o_KTn[h, :, r0:r0 + 128])
                    transpose_out(lambda h, kbt=kbt: kbt[:, 768:832], kbk, 1, 64, lambda h: o_KTr[:, r0:r0 + 128])
                elif name in ("dq0", "dq1", "dk0", "dk1"):
                    isq = name[1] == "q"
                    k = int(name[2])
                    rt, rtk = per_tt[tt]["rope"]
                    st, stk = stats.next()
                    fxt, fxk = fx.next()
                    fxv = fxt[:, 0:384].rearrange("p (h d) -> p h d", h=3)
                    dbt, dbk = qb.next()
                    dbv = dbt[:, 0:384].rearrange("p (h d) -> p h d", h=3)
                    goff = G_DQ if isq else G_DK
                    head_norm(pt[:, 0:384].rearrange("p (h d) -> p h d", h=3), ptk, 3, 128, None, gains[0][:, goff:goff + 128],
                              DIL_SCALE if isq else 1.0, fxv, fxk, st, stk, 0)
                    rope(fxv[:, :, 0:16], fxv[:, :, 16:32], 3, 16, rt[:, 64:80], rt[:, 80:96], rtk, fxk, dbv[:, :, 0:16], dbv[:, :, 16:32], dbk)
                    P.op("act", lambda e, dbv=dbv, fxv=fxv: e.activation(out=dbv[:, :, 32:128], in_=fxv[:, :, 32:128], func=AF.Copy),
                         reads=[fxk], writes=[dbk])
                    dst = o_QTd if isq else o_KTd
                    transpose_out(lambda h, dbv=dbv: dbv[:, h, :], dbk, 3, 128, lambda h, k=k, dst=dst: dst[3 * k + h, :, r0:r0 + 128])
    ctx.end()


DIL_HALO = 1024
DIL_NKT = 20


def dil_masks():
    i = np.arange(DIL_NKT)[:, None, None]
    p = np.arange(128)[None, :, None]
    c = np.arange(512)[None, None, :]
    dlt = 128 * i - DIL_HALO + p - c
    m = (np.abs(dlt) <= 64).astype(np.float32)
    m += ((np.abs(dlt) <= 256) & (dlt % 4 == 0))
    m += ((np.abs(dlt) <= 1024) & (dlt % 16 == 0))
    return m.astype(np.float32)


def emit_ATT(ctx, T, S, QT=512):
    nc, P = ctx.nc, ctx.P
    dt_ = nc.dram_tensor
    NKT = S // 128
    NQ = T // QT
    TH = T + 2 * DIL_HALO
    NHT = TH // 128
    QTn_d = ctx.din("QTn", [6, 128, T], BF16)
    QTr_d = ctx.din("QTr", [6, 64, T], BF16)
    assert S == 4 * T
    KTn_g = ctx.din("KTn_g", [4, 6, 128, T], BF16)
    KTr_g = ctx.din("KTr_g", [4, 64, T], BF16)
    Vb_g = ctx.din("Vb_g", [4, 6, 128, T], BF16)
    QTd_d = ctx.din("QTd", [6, 128, T], BF16)
    KTd_o = ctx.din("KTd_o", [6, 128, T], BF16)
    Vdb_o = ctx.din("Vdb_o", [6, 128, T], BF16)
    KTd_g = ctx.din("KTd_g", [4, 6, 128, T], BF16)
    Vdb_g = ctx.din("Vdb_g", [4, 6, 128, T], BF16)
    sel_d = ctx.din("sel", [128, 12], F32)
    valid_d = ctx.din("valid", [128, NHT], F32)
    mask_d = ctx.din("masks", [128, DIL_NKT, 512], BF16)
    y_mla = ctx.dout("y_mla", [6, 128, T], BF16)
    y_dil = ctx.dout("y_dil", [6, 128, T], BF16)
    cnt = [0]

    def sb(shape, dt, name):
        return ctx.sb(name, shape, dt)

    def ring(n, shape, dt, name):
        return Ring([(sb(shape, dt, f"{name}{i}"), P.tok(f"{name}{i}")) for i in range(n)])

    def psring(n, shape, dt, name):
        return Ring([(ctx.ps(f"{name}{i}", shape, dt), P.tok(f"{name}{i}")) for i in range(n)])

    kbuf = ring(2, [128, S], BF16, "kbuf")
    vbuf = ring(2, [128, NKT * 128], BF16, "vbuf")
    ktr = (sb([64, S], BF16, "ktr"), P.tok())
    qn = ring(2, [128, T], BF16, "qn")
    qr = ring(2, [64, T], BF16, "qr")
    pT = ring(4, [128, QT], BF16, "pT")
    eT = ring(2, [128, QT], BF16, "eT")
    ones = (sb([128, 128], BF16, "ones"), P.tok())
    rden = ring(1, [128, QT], F32, "rden")
    ost = ring(1, [128, QT], BF16, "ost")
    MW = DIL_NKT * 512
    alias_masks = (S - TH) >= MW
    if alias_masks:
        masks_ap = kbuf.items[0][0][:, TH:TH + MW].rearrange("p (i c) -> p i c", i=DIL_NKT)
    else:
        masks_ap = sb([128, DIL_NKT, 512], BF16, "masks")[:]
    masks = (masks_ap, P.tok())
    valid = (sb([128, NHT], F32, "valid"), P.tok())
    sel = (sb([128, 12], F32, "sel"), P.tok())
    htmp = ring(1, [128, 4, DIL_HALO // 4], BF16, "htmp")
    ps_s = psring(4, [128, QT], F32, "ps_s")
    ps_o = psring(2, [128, QT], F32, "ps_o")
    ps_d = psring(2, [128, QT], F32, "ps_d")
    dacc = ring(1, [128, QT], F32, "dacc")
    ones32 = (sb([128, 128], F32, "ones32"), P.tok())
    outs = []

    P.op("pool", lambda e: e.memset(ones[0][:], 1.0), writes=[ones[1]])
    P.op("pool", lambda e: e.memset(ones32[0][:], 1.0), writes=[ones32[1]])
    for i in range(4):
        P.dma("sp", ktr[0][:, i * T:(i + 1) * T], KTr_g[i], writes=[ktr[1]])
    P.dma("sp", sel[0][:], sel_d, writes=[sel[1]])
    P.dma("sp", valid[0][:], valid_d, writes=[valid[1]])

    def finish(po, pok, pd, pdk, dst):
        rd, rdk = rden.next()
        P.op("dve", lambda e: e.reciprocal(out=rd[:], in_=pd[:]), reads=[pdk], writes=[rdk])
        ot, otk = ost.next()
        P.op("dve", lambda e: e.tensor_tensor(out=ot[:], in0=po[:], in1=rd[:], op=ALU.mult), reads=[pok, rdk], writes=[otk])
        P.dma("pool", dst, ot[:], reads=[otk], writes=[], chan=otk)
        if otk not in outs:
            outs.append(otk)

    def load_head(h):
        kt, ktk = kbuf.next()
        vt, vtk = vbuf.next()
        qt, qtk = qn.next()
        qrt, qrk = qr.next()
        P.dma("sp", qt[:], QTn_d[h], writes=[qtk])
        P.dma("sp", qrt[:], QTr_d[h], writes=[qrk])
        for i in range(4):
            P.dma("sp", kt[:, i * T:(i + 1) * T], KTn_g[i, h], writes=[ktk])
            P.dma("pool", vt[:, i * T:(i + 1) * T], Vb_g[i, h], writes=[vtk])
        return kt, ktk, vt, vtk, qt, qtk, qrt, qrk

    nxt = load_head(0)
    for h in range(6):
        kt, ktk, vt, vtk, qt, qtk, qrt, qrk = nxt
        if h + 1 < 6:
            nxt = load_head(h + 1)
        for qi in range(NQ):
            q0 = qi * QT
            po, pok = ps_o.next()
            pd, pdk = ps_d.next()
            def s_step(j):
                ps, psk = ps_s.next()
                P.op("pe", lambda e, ps=ps, kt=kt, qt=qt, j=j, q0=q0: e.matmul(ps[:], kt[:, j * 128:(j + 1) * 128], qt[:, q0:q0 + QT], start=True, stop=False),
                     reads=[ktk, qtk], writes=[psk])
                P.op("pe", lambda e, ps=ps, qrt=qrt, j=j, q0=q0: e.matmul(ps[:], ktr[0][:, j * 128:(j + 1) * 128], qrt[:, q0:q0 + QT], start=False, stop=True),
                     reads=[ktr[1], qrk], writes=[psk])
                pt, ptk = pT.next()
                P.op("act", lambda e, pt=pt, ps=ps: e.activation(out=pt[:], in_=ps[:], func=AF.Exp), reads=[psk], writes=[ptk])
                return pt, ptk

            LOOK = 3
            pend = [s_step(j) for j in range(min(LOOK, NKT))]
            ac, ack = dacc.next()
            for j in range(NKT):
                if j + LOOK < NKT:
                    pend.append(s_step(j + LOOK))
                pt, ptk = pend.pop(0)
                P.op("pe", lambda e, po=po, vt=vt, pt=pt, j=j: e.matmul(po[:], vt[:, j * 128:(j + 1) * 128], pt[:], start=(j == 0), stop=(j == NKT - 1)),
                     reads=[vtk, ptk], writes=[pok])
                if j == 0:
                    P.op("dve", lambda e, ac=ac, pt=pt: e.tensor_copy(out=ac[:], in_=pt[:]), reads=[ptk], writes=[ack])
                else:
                    P.op("dve", lambda e, ac=ac, pt=pt: e.tensor_tensor(out=ac[:], in0=ac[:], in1=pt[:], op=ALU.add), reads=[ptk, ack], writes=[ack])
            P.op("pe", lambda e, pd=pd, ac=ac: e.matmul(pd[:], ones32[0][:], ac[:], start=True, stop=True), reads=[ones32[1], ack], writes=[pdk])
            finish(po, pok, pd, pdk, y_mla[h, :, q0:q0 + QT])

    P.dma("sp", masks[0], mask_d, writes=[masks[1]] + ([kbuf.items[0][1]] if alias_masks else []), chan=masks[1])
    def load_head_d(h):
        kt, ktk = kbuf.next()
        vt, vtk = vbuf.next()
        qt, qtk = qn.next()
        P.dma("sp", qt[:], QTd_d[h], writes=[qtk])
        HH = DIL_HALO
        P.dma("sp", kt[:, HH:HH + T], KTd_o[h], writes=[ktk])
        P.dma("pool", vt[:, HH:HH + T], Vdb_o[h], writes=[vtk])

        def halo(dst_full, dtk, src_g, lo_full, c0):
            for hf_ in range(4):
                HW = HH // 4
                dst = dst_full[:, hf_ * HW:(hf_ + 1) * HW]
                lo = lo_full + hf_ * HW
                tmp, tmk = htmp.next()
                P.dma("sp", tmp[:], src_g[:, h, :, lo:lo + HW].rearrange("r p t -> p r t"), writes=[tmk])
                P.op("dve", lambda e, dst=dst, tmp=tmp: e.tensor_scalar_mul(out=dst, in0=tmp[:, 0, :], scalar1=sel[0][:, c0:c0 + 1]),
                     reads=[tmk, sel[1]], writes=[dtk])
                for r in range(1, 4):
                    P.op("dve", lambda e, r=r, dst=dst, tmp=tmp: e.scalar_tensor_tensor(out=dst, in0=tmp[:, r, :], scalar=sel[0][:, c0 + r:c0 + r + 1], in1=dst,
                                                                                       op0=ALU.mult, op1=ALU.add), reads=[tmk, sel[1], dtk], writes=[dtk])

        halo(kt[:, 0:HH], ktk, KTd_g, T - HH, 4)
        halo(kt[:, HH + T:TH], ktk, KTd_g, 0, 8)
        halo(vt[:, 0:HH], vtk, Vdb_g, T - HH, 4)
        halo(vt[:, HH + T:TH], vtk, Vdb_g, 0, 8)
        return kt, ktk, vt, vtk, qt, qtk

    nxt = load_head_d(0)
    for h in range(6):
        kt, ktk, vt, vtk, qt, qtk = nxt
        if h + 1 < 6:
            nxt = load_head_d(h + 1)
        for qi in range(NQ):
            q0 = qi * QT
            po, pok = ps_o.next()
            pd, pdk = ps_d.next()
            def d_step(i):
                j = qi * (QT // 128) + i
                ps, psk = ps_s.next()
                P.op("pe", lambda e, ps=ps, kt=kt, qt=qt, j=j, q0=q0: e.matmul(ps[:], kt[:, j * 128:(j + 1) * 128], qt[:, q0:q0 + QT], start=True, stop=True),
                     reads=[ktk, qtk], writes=[psk])
                et, etk = eT.next()
                P.op("act", lambda e, et=et, ps=ps: e.activation(out=et[:], in_=ps[:], func=AF.Exp), reads=[psk], writes=[etk])
                pt, ptk = pT.next()
                P.op("dve", lambda e, pt=pt, et=et, i=i, j=j: e.scalar_tensor_tensor(
                    out=pt[:], in0=et[:], scalar=valid[0][:, j:j + 1], in1=masks[0][:, i, :], op0=ALU.mult, op1=ALU.mult),
                    reads=[etk, valid[1], masks[1]], writes=[ptk])
                return pt, ptk, j

            LOOK = 2
            pend = [d_step(i) for i in range(LOOK)]
            for i in range(DIL_NKT):
                if i + LOOK < DIL_NKT:
                    pend.append(d_step(i + LOOK))
                pt, ptk, j = pend.pop(0)
                P.op("pe", lambda e, po=po, vt=vt, pt=pt, j=j, i=i: e.matmul(po[:], vt[:, j * 128:(j + 1) * 128], pt[:], start=(i == 0), stop=(i == DIL_NKT - 1)),
                     reads=[vtk, ptk], writes=[pok])
                P.op("pe", lambda e, pd=pd, pt=pt, i=i: e.matmul(pd[:], ones[0][:], pt[:], start=(i == 0), stop=(i == DIL_NKT - 1)),
                     reads=[ones[1], ptk], writes=[pdk])
            finish(po, pok, pd, pdk, y_dil[h, :, q0:q0 + QT])
    ctx.end()


def emit_ML(ctx, S):
    nc, P = ctx.nc, ctx.P
    dt_ = nc.dram_tensor
    NJ = S // 128
    T4 = S // 4
    NJc = T4 // 128
    qT_g = ctx.din("qT_g", [4, 4, 64, T4], BF16)
    kT_g = ctx.din("kT_g", [4, 4, 64, T4], BF16)
    kb_g = ctx.din("kb_g", [4, 4, 128, NJc, 64], BF16)
    vb_g = ctx.din("vb_g", [4, 4, 128, NJc, 128], BF16)
    osb_g = ctx.din("osb_g", [4, 4, 2, 64, NJc, 128], F32)
    gb_g = ctx.din("gb_g", [4, 128, NJc, 16], F32)
    sel_d = ctx.din("sel", [128, 12], F32)
    bias_d = ctx.din("bias", [128, 4], F32)
    gain_d = ctx.din("gain", [128, 128], F32)
    tri_d = ctx.din("tri", [128, 3, 128], F32)
    ya_d = ctx.dout("ya", [4, 128, S // 512, 128], BF16)

    def sb(shape, dt, name):
        return ctx.sb(name, shape, dt)

    def ring(n, shape, dt, name):
        return Ring([(sb(shape, dt, f"{name}{i}"), P.tok(f"{name}{i}")) for i in range(n)])

    def psring(n, shape, dt, name):
        return Ring([(ctx.ps(f"{name}{i}", shape, dt), P.tok(f"{name}{i}")) for i in range(n)])

    qT = (sb([64, S], BF16, "qT"), P.tok())
    kT = (sb([64, S], BF16, "kT"), P.tok())
    kb = (sb([128, NJ, 64], BF16, "kb"), P.tok())
    vb = (sb([128, NJ, 128], BF16, "vb"), P.tok())
    gb = (sb([128, NJ, 4], F32, "gb"), P.tok())
    bias = (sb([128, 4], F32, "bias"), P.tok())
    gain = (sb([128, 128], F32, "gain"), P.tok())
    tri = (sb([128, 3, 128], F32, "tri"), P.tok())
    trib = (sb([128, 2, 128], BF16, "trib"), P.tok())
    hf = (sb([128, NJ, 128], BF16, "hf"), P.toks(NJ))
    lf = [(sb([128, NJ], F32, f"lf{d}"), P.tok()) for d in range(2)]
    ea = [(sb([128, NJ], F32, f"ea{d}"), P.tok()) for d in range(2)]
    eb = [(sb([128, NJ], F32, f"eb{d}"), P.tok()) for d in range(2)]
    et = [(sb([128, NJ], F32, f"et{d}"), P.tok()) for d in range(2)]
    C32 = [(sb([64, 130], F32, f"C32_{d}"), P.tok()) for d in range(2)]
    Cb = [ring(2, [64, 130], BF16, f"Cb{d}") for d in range(2)]
    vx = ring(3, [128, 130], BF16, "vx")
    sm = ring(3, [128, 128], BF16, "sm")
    o32 = ring(3, [128, 130], F32, "o32")
    st = ring(4, [128, 4], F32, "st")
    hs = ring(2, [128, 128], F32, "hs")
    sq = ring(2, [128, 128], F32, "sqm")
    og = ring(2, [128, 128], F32, "og")
    og4 = (sb([128, 512], F32, "og4"), P.tok())
    yo = ring(3, [128, 128], BF16, "yo")
    ps_g = psring(2, [128, 512], F32, "ps_g")
    ps_s = psring(2, [128, 128], F32, "ps_s")
    ps_o = psring(2, [128, 130], F32, "ps_o")
    ps_u = psring(2, [64, 130], F32, "ps_u")
    outs = []

    sel = (sb([128, 12], F32, "sel"), P.tok())
    gfull = (sb([128, NJ, 16], F32, "gfull"), P.tok())
    stmp = ring(1, [128, 4, 1024], BF16, "stmp")
    stmp32 = ring(1, [128, 4, 512], F32, "stmp32")
    P.dma("sp", sel[0][:], sel_d, writes=[sel[1]])

    def select(dst, dtk, tmp, tmk, np_=128):
        P.op("dve", lambda e: e.tensor_scalar_mul(out=dst, in0=tmp[0:np_, 0], scalar1=sel[0][0:np_, 0:1]), reads=[tmk, sel[1]], writes=[dtk])
        for h in range(1, 4):
            P.op("dve", lambda e, h=h: e.scalar_tensor_tensor(out=dst, in0=tmp[0:np_, h], scalar=sel[0][0:np_, h:h + 1], in1=dst,
                                                               op0=ALU.mult, op1=ALU.add), reads=[tmk, sel[1], dtk], writes=[dtk])

    for r in range(4):
        for c in range(T4 // 1024):
            for (dstt, srcg) in ((qT, qT_g), (kT, kT_g)):
                tmp, tmk = stmp.next()
                P.dma("sp", tmp[0:64], srcg[r, :, :, c * 1024:(c + 1) * 1024].rearrange("h p t -> p h t"), writes=[tmk])
                select(dstt[0][:, r * T4 + c * 1024:r * T4 + (c + 1) * 1024], dstt[1], tmp, tmk, 64)
        for c in range(NJc // 16):
            tmp, tmk = stmp.next()
            tv = tmp[:].rearrange("p h (j d) -> p h j d", d=64)
            P.dma("pool", tv, kb_g[r, :, :, c * 16:(c + 1) * 16, :].rearrange("h p j d -> p h j d"), writes=[tmk])
            select(kb[0][:, r * NJc + c * 16:r * NJc + (c + 1) * 16, :], kb[1], tv, tmk)
        for c in range(NJc // 8):
            tmp, tmk = stmp.next()
            tv = tmp[:].rearrange("p h (j d) -> p h j d", d=128)
            P.dma("pool", tv, vb_g[r, :, :, c * 8:(c + 1) * 8, :].rearrange("h p j d -> p h j d"), writes=[tmk])
            select(vb[0][:, r * NJc + c * 8:r * NJc + (c + 1) * 8, :], vb[1], tv, tmk)
        P.dma("sp", gfull[0][:, r * NJc:(r + 1) * NJc, :], gb_g[r], writes=[gfull[1]])
    for k in range(4):
        base = (k // 2) * 8 + (k % 2) * 4
        P.op("dve", lambda e, k=k, base=base: e.tensor_scalar_mul(out=gb[0][:, :, k], in0=gfull[0][:, :, base], scalar1=sel[0][:, 0:1]),
             reads=[gfull[1], sel[1]], writes=[gb[1]])
        for h in range(1, 4):
            P.op("dve", lambda e, k=k, base=base, h=h: e.scalar_tensor_tensor(out=gb[0][:, :, k], in0=gfull[0][:, :, base + h], scalar=sel[0][:, h:h + 1],
                                                                             in1=gb[0][:, :, k], op0=ALU.mult, op1=ALU.add),
                 reads=[gfull[1], sel[1], gb[1]], writes=[gb[1]])
    P.dma("sp", bias[0][:], bias_d, writes=[bias[1]])
    P.dma("sp", gain[0][:], gain_d, writes=[gain[1]])
    P.dma("sp", tri[0][:], tri_d, writes=[tri[1]])
    P.op("dve", lambda e: e.tensor_copy(out=trib[0][:], in_=tri[0][:, 0:2, :]), reads=[tri[1]], writes=[trib[1]])

    for d in range(2):
        gi = gb[0][:, :, 2 * d]
        gf = gb[0][:, :, 2 * d + 1]
        lft, lfk = lf[d]
        eat, eak = ea[d]
        ebt, ebk = eb[d]
        ett, etk = et[d]
        P.op("dve", lambda e, lft=lft, gf=gf, d=d: e.tensor_scalar(out=lft[:], in0=gf, scalar1=bias[0][:, 2 * d + 1:2 * d + 2], scalar2=-1.0,
                                                                   op0=ALU.add, op1=ALU.mult), reads=[gb[1], bias[1]], writes=[lfk])
        P.op("act", lambda e, lft=lft: e.activation(out=lft[:], in_=lft[:], func=AF.Exp), reads=[lfk], writes=[lfk])
        P.op("dve", lambda e, lft=lft: e.tensor_scalar_add(out=lft[:], in0=lft[:], scalar1=1.0), reads=[lfk], writes=[lfk])
        P.op("act", lambda e, lft=lft: e.activation(out=lft[:], in_=lft[:], func=AF.Ln), reads=[lfk], writes=[lfk])
        P.op("dve", lambda e, lft=lft: e.tensor_scalar_mul(out=lft[:], in0=lft[:], scalar1=-1.0), reads=[lfk], writes=[lfk])
        pg, pgk = ps_g.next()
        P.op("pe", lambda e, pg=pg, lft=lft, d=d: e.matmul(pg[:, 0:NJ], tri[0][:, d, :], lft[:], start=True, stop=True),
             reads=[tri[1], lfk], writes=[pgk])
        P.op("pe", lambda e, pg=pg, lft=lft: e.matmul(pg[:, 256:256 + NJ], tri[0][:, 2, :], lft[:], start=True, stop=True),
             reads=[tri[1], lfk], writes=[pgk])
        P.op("dve", lambda e, eat=eat, gi=gi, pg=pg, d=d: e.scalar_tensor_tensor(
            out=eat[:], in0=gi, scalar=bias[0][:, 2 * d:2 * d + 1], in1=pg[:, 0:NJ], op0=ALU.add, op1=ALU.subtract),
            reads=[gb[1], bias[1], pgk], writes=[eak])
        P.op("act", lambda e, eat=eat: e.activation(out=eat[:], in_=eat[:], func=AF.Exp), reads=[eak], writes=[eak])
        P.op("act", lambda e, ebt=ebt, pg=pg: e.activation(out=ebt[:], in_=pg[:, 0:NJ], func=AF.Exp), reads=[pgk], writes=[ebk])
        P.op("dve", lambda e, ebt=ebt: e.tensor_scalar_mul(out=ebt[:], in0=ebt[:], scalar1=0.125), reads=[ebk], writes=[ebk])
        P.op("act", lambda e, ett=ett, pg=pg: e.activation(out=ett[:], in_=pg[:, 256:256 + NJ], func=AF.Exp), reads=[pgk], writes=[etk])
        P.op("pool", lambda e, d=d: e.memset(C32[d][0][:], 0.0), writes=[C32[d][1]])

    def chunk(d, j, first):
        eat, eak = ea[d]
        ebt, ebk = eb[d]
        ett, etk = et[d]
        c0 = j * 128
        ps, psk = ps_s.next()
        P.op("pe", lambda e: e.matmul(ps[:], kT[0][:, c0:c0 + 128], qT[0][:, c0:c0 + 128], start=True, stop=True),
             reads=[kT[1], qT[1]], writes=[psk])
        smt, smk = sm.next()
        P.op("dve", lambda e: e.tensor_tensor(out=smt[:], in0=ps[:], in1=trib[0][:, d, :], op=ALU.mult), reads=[psk, trib[1]], writes=[smk])
        vxt, vxk = vx.next()
        P.op("pool", lambda e: e.tensor_scalar_mul(out=vxt[:, 0:128], in0=vb[0][:, j, :], scalar1=eat[:, j:j + 1]), reads=[vb[1], eak], writes=[vxk])
        P.op("pool", lambda e: e.tensor_copy(out=vxt[:, 128:129], in_=eat[:, j:j + 1]), reads=[eak], writes=[vxk])
        po, pok = ps_o.next()
        P.op("pe", lambda e: e.matmul(po[:, 0:129], smt[:], vxt[:, 0:129], start=True, stop=first), reads=[smk, vxk], writes=[pok])
        if not first:
            cbt, cbk = Cb[d].items[(Cb[d].i - 1) % 2]
            P.op("pe", lambda e: e.matmul(po[:, 0:129], qT[0][:, c0:c0 + 128], cbt[:, 0:129], start=False, stop=True),
                 reads=[qT[1], cbk], writes=[pok])
        pu, puk = ps_u.next()
        P.op("pe", lambda e: e.matmul(pu[:, 0:129], kb[0][:, j, :], vxt[:, 0:129], start=True, stop=True), reads=[kb[1], vxk], writes=[puk])
        c32, c32k = C32[d]
        P.op("dve", lambda e: e.tensor_tensor(out=c32[:, 0:129], in0=c32[:, 0:129], in1=pu[:, 0:129], op=ALU.add), reads=[c32k, puk], writes=[c32k])
        P.op("dve", lambda e: e.tensor_scalar_mul(out=c32[:, 0:129], in0=c32[:, 0:129], scalar1=ett[0:64, j:j + 1]), reads=[c32k, etk], writes=[c32k])
        cbn, cbnk = Cb[d].next()
        P.op("act", lambda e: e.activation(out=cbn[:, 0:129], in_=c32[:, 0:129], func=AF.Copy), reads=[c32k], writes=[cbnk])
        ot, otk = o32.next()
        P.op("act", lambda e: e.activation(out=ot[:, 0:129], in_=po[:, 0:129], func=AF.Copy, scale=ebt[:, j:j + 1]), reads=[pok, ebk], writes=[otk])
        s4, s4k = st.next()
        P.op("dve", lambda e: e.scalar_tensor_tensor(out=s4[:, 0:1], in0=ot[:, 128:129], scalar=-1.0, in1=ot[:, 128:129], op0=ALU.mult, op1=ALU.max),
             reads=[otk], writes=[s4k])
        P.op("dve", lambda e: e.tensor_scalar_max(out=s4[:, 0:1], in0=s4[:, 0:1], scalar1=1.0), reads=[s4k], writes=[s4k])
        P.op("dve", lambda e: e.reciprocal(out=s4[:, 0:1], in_=s4[:, 0:1]), reads=[s4k], writes=[s4k])
        return ot, otk, s4, s4k

    for j in range(NJ):
        ot, otk, s4, s4k = chunk(0, j, j == 0)
        P.op("dve", lambda e, ot=ot, s4=s4, j=j: e.tensor_scalar_mul(out=hf[0][:, j, :], in0=ot[:, 0:128], scalar1=s4[:, 0:1]),
             reads=[otk, s4k], writes=[hf[1][j]])
    for jj in range(NJ):
        j = NJ - 1 - jj
        ot, otk, s4, s4k = chunk(1, j, jj == 0)
        ht, htk = hs.next()
        P.op("dve", lambda e, ot=ot, s4=s4, j=j, ht=ht: e.scalar_tensor_tensor(out=ht[:], in0=ot[:, 0:128], scalar=s4[:, 0:1], in1=hf[0][:, j, :],
                                                                               op0=ALU.mult, op1=ALU.add), reads=[otk, s4k, hf[1][j]], writes=[htk])
        sqt, sqk = sq.next()
        P.op("act", lambda e, sqt=sqt, ht=ht, s4=s4: e.activation(out=sqt[:], in_=ht[:], func=AF.Square, scale=128.0 ** -0.5, accum_out=s4[:, 1:2]),
             reads=[htk], writes=[sqk, s4k])
        P.op("dve", lambda e, s4=s4: e.tensor_scalar_add(out=s4[:, 1:2], in0=s4[:, 1:2], scalar1=EPS), reads=[s4k], writes=[s4k])
        P.op("act", lambda e, s4=s4: e.activation(out=s4[:, 1:2], in_=s4[:, 1:2], func=AF.Ln), reads=[s4k], writes=[s4k])
        P.op("act", lambda e, s4=s4: e.activation(out=s4[:, 1:2], in_=s4[:, 1:2], func=AF.Exp, scale=-0.5), reads=[s4k], writes=[s4k])
        ogt, ogk = og.next()
        if jj % 4 == 0:
            j_lo = j - 3
            rr, jl = j_lo // NJc, j_lo % NJc
            t32, t32k = stmp32.next()
            tv32 = t32[:].rearrange("p h (j d) -> p h j d", d=128)
            for c2 in range(2):
                P.dma("sp", tv32[c2 * 64:(c2 + 1) * 64], osb_g[rr, :, c2, :, jl:jl + 4, :].rearrange("h p j d -> p h j d"), writes=[t32k])
            select(og4[0][:].rearrange("p (j d) -> p j d", d=128), og4[1], tv32, t32k)
        P.op("act", lambda e, ogt=ogt, j=j: e.activation(out=ogt[:], in_=og4[0][:, ((j % 4)) * 128:((j % 4) + 1) * 128], func=AF.Copy),
             reads=[og4[1]], writes=[ogk])
        P.op("pool", lambda e, ogt=ogt: e.tensor_tensor(out=ogt[:], in0=ogt[:], in1=gain[0][:], op=ALU.mult), reads=[ogk, gain[1]], writes=[ogk])
        yt, ytk = yo.next()
        P.op("dve", lambda e, yt=yt, ht=ht, s4=s4, ogt=ogt: e.scalar_tensor_tensor(out=yt[:], in0=ht[:], scalar=s4[:, 1:2], in1=ogt[:],
                                                                                   op0=ALU.mult, op1=ALU.mult), reads=[htk, s4k, ogk], writes=[ytk])
        P.dma("pool", ya_d[j // NJc, :, j % NJc, :], yt[:], reads=[ytk], writes=[], chan=ytk)
        if ytk not in outs:
            outs.append(ytk)
    ctx.end()


def emit_YA(ctx, T, S):
    nc, P = ctx.nc, ctx.P
    NJ = S // 128
    NJc = T // 128
    ya_g = ctx.din("ya_g", [4, 128, 4, T], BF16)
    sel_d = ctx.din("sel", [128, 12], F32)
    ident_d = ctx.din("ident", [128, 128], BF16)
    yT = ctx.dout("yT", [128, 16, T], BF16)
    sel = (ctx.sb("sel", [128, 12], F32), P.tok())
    ident = (ctx.sb("ident", [128, 128], BF16), P.tok())
    tmp = Ring([(ctx.sb(f"t{i}", [128, 4, 1024], BF16), P.tok()) for i in range(2)])
    sl = Ring([(ctx.sb(f"sl{i}", [128, 8, 128], BF16), P.tok()) for i in range(2)])
    ob = Ring([(ctx.sb(f"ob{i}", [128, 1024], BF16), P.tok()) for i in range(2)])
    pt = Ring([(ctx.ps(f"pt{i}", [128, 1024], BF16), P.tok()) for i in range(2)])
    P.dma("sp", sel[0][:], sel_d, writes=[sel[1]])
    P.dma("sp", ident[0][:], ident_d, writes=[ident[1]])
    for h in range(4):
        for c in range(NJc // 8):
            t, tk = tmp.next()
            tv = t[:].rearrange("p s (j d) -> p s j d", d=128)
            P.dma("sp", t[:], ya_g[h][:, :, c * 1024:(c + 1) * 1024], writes=[tk])
            st, stk = sl.next()
            P.op("dve", lambda e, st=st, tv=tv: e.tensor_scalar_mul(out=st[:], in0=tv[:, 0], scalar1=sel[0][:, 0:1]), reads=[tk, sel[1]], writes=[stk])
            for r in range(1, 4):
                P.op("dve", lambda e, st=st, tv=tv, r=r: e.scalar_tensor_tensor(out=st[:], in0=tv[:, r], scalar=sel[0][:, r:r + 1], in1=st[:],
                                                                                op0=ALU.mult, op1=ALU.add), reads=[tk, sel[1], stk], writes=[stk])
            p_, pk = pt.next()
            for j in range(8):
                P.op("pe", lambda e, p_=p_, st=st, j=j: e.transpose(p_[:, j * 128:(j + 1) * 128], st[:, j, :], ident[0][:]),
                     reads=[stk, ident[1]], writes=[pk])
            o_, ok = ob.next()
            P.op("act", lambda e, o_=o_, p_=p_: e.activation(out=o_[:], in_=p_[:], func=AF.Copy), reads=[pk], writes=[ok])
            P.dma("pool", yT[:, h, c * 1024:(c + 1) * 1024], o_[:], reads=[ok], writes=[], chan=ok)
    ctx.end()


XCH = (("KTn", 768), ("Vb", 768), ("KTd", 768), ("Vdb", 768), ("qT", 512), ("kT", 512), ("vb", 512), ("kb", 512), ("KTr", 128))
XCH_ROWS = sum(n for _, n in XCH)
W_PIECE = 128 * 4096


def build_fused(S, NI, debug=False):
    T = S // 4
    assert T == 4096
    NJ, NJc = S // 128, T // 128
    nc = bass.Bass("TRN2", target_bir_lowering=False)
    P = Prog(nc)
    G4 = [[0, 1, 2, 3], [4, 5, 6, 7]]

    def ext_in(name, shape, dt):
        return nc.dram_tensor(name, list(shape), dt, kind="ExternalInput").ap()

    def scratch(name, shape, dt):
        return nc.dram_tensor(name, list(shape), dt).ap()

    def gather(src2d, dst2d):
        P.collective("AllGather", src2d.opt(), dst2d.opt(), G4)

    wsrc = ext_in("wsrc", [NI * 128, 4096], F32)
    xT_in = ext_in("xT", [128, 16, T], F32)
    rope_in = ext_in("rope", [T, 96], F32)
    ident_in = ext_in("ident", [128, 128], BF16)
    masks_in = ext_in("masks", [128, DIL_NKT, 512], BF16)
    valid_in = ext_in("valid", [128, (T + 2 * DIL_HALO) // 128], F32)
    sel_in = ext_in("sel", [128, 12], F32)
    tri_in = ext_in("tri", [128, 3, 128], F32)
    lay = []
    for l in range(DEPTH):
        lay.append(dict(
            gmix=ext_in(f"gmix{l}", [128, 16], F32), wq=ext_in(f"wq{l}", [128, 3, 1152], F32), wkv=ext_in(f"wkv{l}", [128, 1536], F32),
            qlat_g=ext_in(f"qlat_g{l}", [128, 3], F32), kvlat_g=ext_in(f"kvlat_g{l}", [128, 1], F32), gains=ext_in(f"gains{l}", [128, 704], F32),
            bias=ext_in(f"mlbias{l}", [128, 4], F32), gain=ext_in(f"mlgain{l}", [128, 128], F32), gff=ext_in(f"gff{l}", [128, 16], F32)))
    outT = nc.dram_tensor("outT", [128, 16, T], F32, kind="ExternalOutput").ap()

    wpart = scratch("wpart", [NI * 128, 4096], BF16)
    wall = scratch("wall", [NI * 512, 4096], BF16)
    ctx = Ctx(nc, P, bind={"src": wsrc.rearrange("(i p) c -> p i c", p=128), "dst": wpart.rearrange("(i p) c -> p i c", p=128)}, tag="w_")
    emit_cast(ctx, NI)
    n_first = -(-(D_MODEL * D_IN) // (4 * W_PIECE))
    for i in range(n_first):
        gather(wpart[i * 128:(i + 1) * 128], wall[i * 512:(i + 1) * 512])
    for i in range(n_first, NI):
        gather(wpart[i * 128:(i + 1) * 128], wall[i * 512:(i + 1) * 512])
    P.barrier()
    wflat = wall.rearrange("a b -> (a b)")
    szs = [D_MODEL * D_IN, D_MODEL * D_MODEL, D_MODEL * 4 * D_MODEL, 4 * D_MODEL * D_MODEL]
    off = 0
    for l in range(DEPTH):
        lay[l]["win"] = wflat[off:off + szs[0]].rearrange("(p k n) -> p k n", p=128, k=16)
        off += szs[0]
        lay[l]["wout_b"] = wflat[off:off + szs[1]].rearrange("(c p n) -> c p n", c=16, p=128)
        off += szs[1]
        lay[l]["w1_b"] = wflat[off:off + szs[2]].rearrange("(c p n) -> c p n", c=64, p=128)
        off += szs[2]
        lay[l]["w2_b"] = wflat[off:off + szs[3]].rearrange("(c p n) -> c p n", c=64, p=128)
        off += szs[3]

    NCH = XCH_ROWS // 128
    xch = scratch("xch", [XCH_ROWS, T], BF16)
    xg = scratch("xg", [NCH, 4, 128, T], BF16)
    osb_o = scratch("osb_o", [512, T], F32)
    osb_gs = scratch("osb_gs", [8, 4, 64, T], F32)
    gb_o = scratch("gb_o", [16, T], F32)
    gb_gs = scratch("gb_gs", [4 * 16, T], F32)
    QTn = scratch("QTn", [6, 128, T], BF16)
    QTr = scratch("QTr", [6, 64, T], BF16)
    QTd = scratch("QTd", [6, 128, T], BF16)
    ya_o = scratch("ya_o", [4, 128, T], BF16)
    ya_gs = scratch("ya_gs", [4, 4 * 128, T], BF16)
    if debug:
        yT = nc.dram_tensor("dbg_yT", [128, 16, T], BF16, kind="ExternalOutput").ap()
        x1T = nc.dram_tensor("dbg_x1T", [128, 16, T], F32, kind="ExternalOutput").ap()
        yT1 = scratch("yT1", [128, 16, T], BF16)
    else:
        yT = scratch("yT", [128, 16, T], BF16)
        x1T = scratch("x1T", [128, 16, T], F32)
        yT1 = yT
    R = {}
    a = 0
    for k, n in XCH:
        R[k] = (a, a + n)
        a += n

    def own(k):
        a, b = R[k]
        return xch[a:b]

    def gat(k):
        a, b = R[k]
        return xg[a // 128:b // 128]

    def hp(ap):
        return ap.rearrange("(h p) t -> h p t", p=128)

    def ghp(k):
        return gat(k).rearrange("h r p t -> r h p t")

    for l in range(DEPTH):
        L = lay[l]
        x_in = xT_in if l == 0 else x1T
        x_out = x1T if l == 0 else outT
        yT_l = yT if l == 0 else yT1
        bindA = dict(xT=x_in, win=L["win"], gmix=L["gmix"], wq=L["wq"], wkv=L["wkv"], qlat_g=L["qlat_g"], kvlat_g=L["kvlat_g"], gains=L["gains"],
                     rope=rope_in, ident=ident_in,
                     o_qTml=hp(own("qT"))[:, 0:64, :], o_kTml=hp(own("kT"))[:, 0:64, :],
                     o_kb=hp(own("kb"))[:, :, 0:NJc * 64].rearrange("h p (j d) -> h p j d", d=64),
                     o_vb=hp(own("vb")).rearrange("h p (j d) -> h p j d", d=128),
                     o_osb=hp(osb_o).rearrange("h p (j d) -> h p j d", d=128),
                     o_gb=gb_o.rearrange("q t -> (q t)").rearrange("(p j d) -> p j d", p=128, d=16),
                     o_QTn=QTn, o_QTr=QTr, o_KTn=hp(own("KTn")), o_KTr=own("KTr")[0:64],
                     o_Vb=hp(own("Vb")), o_QTd=QTd, o_KTd=hp(own("KTd")), o_Vdb=hp(own("Vdb")))
        emit_A(Ctx(nc, P, bind=bindA, tag=f"a{l}_"), T)
        ml_chunks = [i for k in ("qT", "kT", "vb", "kb") for i in range(R[k][0] // 128, R[k][1] // 128)]
        for i in ml_chunks:
            gather(xch[i * 128:(i + 1) * 128], xg[i].rearrange("r p t -> (r p) t"))
        for i in range(8):
            gather(osb_o[i * 64:(i + 1) * 64], osb_gs[i].rearrange("r p t -> (r p) t"))
        gather(gb_o, gb_gs)
        for i in range(NCH):
            if i not in ml_chunks:
                gather(xch[i * 128:(i + 1) * 128], xg[i].rearrange("r p t -> (r p) t"))
        P.barrier()
        bindM = dict(qT_g=ghp("qT")[:, :, 0:64, :], kT_g=ghp("kT")[:, :, 0:64, :],
                     kb_g=ghp("kb")[:, :, :, 0:NJc * 64].rearrange("r h p (j d) -> r h p j d", d=64),
                     vb_g=ghp("vb").rearrange("r h p (j d) -> r h p j d", d=128),
                     osb_g=osb_gs.rearrange("(h c) r q (j d) -> r h c q j d", c=2, d=128),
                     gb_g=gb_gs.rearrange("(r q) t -> r (q t)", r=4).rearrange("r (p j d) -> r p j d", p=128, d=16),
                     sel=sel_in, bias=L["bias"], gain=L["gain"], tri=tri_in, ya=ya_o.rearrange("s p (j d) -> s p j d", d=128))
        emit_ML(Ctx(nc, P, bind=bindM, tag=f"m{l}_"), S)
        for sg in range(4):
            gather(ya_o[sg], ya_gs[sg])
        bindT = dict(QTn=QTn, QTr=QTr, QTd=QTd, KTn_g=ghp("KTn"), KTr_g=gat("KTr")[0][:, 0:64, :], Vb_g=ghp("Vb"),
                     KTd_o=hp(own("KTd")), Vdb_o=hp(own("Vdb")), KTd_g=ghp("KTd"), Vdb_g=ghp("Vdb"),
                     sel=sel_in, valid=valid_in, masks=masks_in,
                     y_mla=yT_l.rearrange("p k t -> k p t")[4:10], y_dil=yT_l.rearrange("p k t -> k p t")[10:16])
        emit_ATT(Ctx(nc, P, bind=bindT, tag=f"t{l}_"), T, S)
        emit_YA(Ctx(nc, P, bind=dict(ya_g=ya_gs.rearrange("s (h p) n -> h p s n", p=128), sel=sel_in, ident=ident_in, yT=yT_l), tag=f"y{l}_"), T, S)
        bindC = dict(xT=x_in, yT=yT_l, wout_b=L["wout_b"], w1_b=L["w1_b"], w2_b=L["w2_b"], gff=L["gff"], xoT=x_out)
        emit_C(Ctx(nc, P, bind=bindC, tag=f"c{l}_"), T)
    P.finalize()
    P.emit()
    return nc, P


def _rope_tab(seq, rot):
    pos = np.arange(seq, dtype=np.float32)
    inv = (np.float32(500000.0) ** (-np.arange(0, rot, 2, dtype=np.float32) / np.float32(rot))).astype(np.float32)
    ang = (pos[:, None] * inv[None, :]).astype(np.float32)
    return np.cos(ang).astype(np.float32), np.sin(ang).astype(np.float32)


def _blk(w):
    Kd, N = w.shape
    return np.ascontiguousarray(w.reshape(Kd // 128, 128, N // 128, 128).transpose(2, 1, 0, 3)).reshape(N // 128, 128, Kd)


def _to_bf16_exact(a):
    import ml_dtypes
    u = np.ascontiguousarray(a, np.float32).view(np.uint32)
    return (u >> 16).astype(np.uint16).view(ml_dtypes.bfloat16)


_DEBUG = {"on": True}


def kernel(x, norm_mix, w_in, ml_i_bias, ml_f_bias, ml_out_norm, mla_q_norm, mla_w_q_b, mla_kv_norm, mla_w_kv_b,
           mla_q_head_norm, mla_k_head_norm, dil_q_norm, dil_k_norm, w_out, norm_ff, w_ff1, w_ff2):
    f32 = np.float32
    x = np.asarray(x, f32)
    B, S, D = x.shape
    T = S // 4
    parts = []
    for l in range(DEPTH):
        wi = np.asarray(w_in[l], f32)
        parts.append(np.ascontiguousarray(wi.reshape(16, 128, D_IN).transpose(1, 0, 2)).reshape(-1))
        parts.append(_blk(np.asarray(w_out[l], f32)).reshape(-1))
        parts.append(_blk(np.asarray(w_ff1[l], f32)).reshape(-1))
        w2 = np.asarray(w_ff2[l], f32)
        parts.append(np.concatenate([_blk(w2[q * D:(q + 1) * D]) for q in range(4)], 0).reshape(-1))
    flat = np.concatenate(parts)
    del parts
    NI = -(-flat.size // (4 * W_PIECE))
    pad = NI * 4 * W_PIECE - flat.size
    if pad:
        flat = np.concatenate([flat, np.zeros(pad, f32)])
    flat = flat.reshape(NI, 4, 128, 4096)
    cb, sb_ = _rope_tab(S, 64)
    cc_, sc_ = _rope_tab(S, 32)
    rope_all = np.concatenate([cb, sb_, cc_, sc_], 1).astype(f32)
    tri = np.zeros((128, 3, 128), f32)
    s_ = np.arange(128)[:, None]
    t_ = np.arange(128)[None, :]
    tri[:, 0] = (s_ <= t_)
    tri[:, 1] = (s_ >= t_)
    tri[:, 2] = 1.0
    ident_b = _to_bf16_exact(np.eye(128, dtype=f32))
    masks_b = _to_bf16_exact(np.ascontiguousarray(dil_masks().transpose(1, 0, 2)))
    H = DIL_HALO
    in_maps = []
    for c in range(NCORES):
        b, r = c // 4, c % 4
        tok = np.arange(r * T - H, (r + 1) * T + H)
        sel = np.zeros((128, 12), f32)
        sel[:, r] = 1.0
        if r > 0:
            sel[:, 4 + r - 1] = 1.0
        if r < 3:
            sel[:, 8 + r + 1] = 1.0
        m = {"wsrc": flat[:, r].reshape(NI * 128, 4096),
             "xT": x[b, r * T:(r + 1) * T].T.reshape(16, 128, T).transpose(1, 0, 2),
             "rope": rope_all[r * T:(r + 1) * T], "ident": ident_b, "masks": masks_b,
             "valid": ((tok >= 0) & (tok < S)).astype(f32).reshape(-1, 128).T, "sel": sel, "tri": tri}
        for l in range(DEPTH):
            gains = np.zeros((128, 704), f32)
            gains[:, 0:192] = mla_q_head_norm[l]
            gains[:, 192:384] = mla_k_head_norm[l]
            gains[:, 384:512] = dil_q_norm[l]
            gains[:, 512:640] = dil_k_norm[l]
            h = r
            bias = np.array([ml_i_bias[l][0, h], ml_f_bias[l][0, h], ml_i_bias[l][1, h], ml_f_bias[l][1, h]], f32)
            m.update({f"gmix{l}": np.asarray(norm_mix[l], f32).reshape(16, 128).T,
                      f"wq{l}": np.asarray(mla_w_q_b[l], f32).reshape(3, 128, 1152).transpose(1, 0, 2),
                      f"wkv{l}": np.asarray(mla_w_kv_b[l], f32), f"qlat_g{l}": np.asarray(mla_q_norm[l], f32).reshape(3, 128).T,
                      f"kvlat_g{l}": np.asarray(mla_kv_norm[l], f32).reshape(128, 1), f"gains{l}": gains,
                      f"mlbias{l}": np.tile(bias[None], (128, 1)), f"mlgain{l}": np.tile(np.asarray(ml_out_norm[l][h], f32)[None], (128, 1)),
                      f"gff{l}": np.asarray(norm_ff[l], f32).reshape(16, 128).T})
        in_maps.append({k: np.ascontiguousarray(v) for k, v in m.items()})
    nc, _ = build_fused(S, NI, debug=_DEBUG["on"])
    res = run_bass_kernel_spmd(nc, in_maps, core_ids=list(range(NCORES))).results
    _DEBUG["res"] = res
    out = np.empty((B, S, D), f32)
    for c in range(NCORES):
        out[c // 4, (c % 4) * T:(c % 4 + 1) * T] = res[c]["outT"].transpose(1, 0, 2).reshape(D, T).T
    return out
```

```python
import numpy as np
import concourse.bass as bass
import concourse.mybir as mybir
from concourse.bass_utils import run_bass_kernel_spmd

F32 = mybir.dt.float32
BF16 = mybir.dt.bfloat16
ALU = mybir.AluOpType
AF = mybir.ActivationFunctionType
AX = mybir.AxisListType

NCORES = 8
D_MODEL = 2048
SEQ = 16384
BATCH = 2
DEPTH = 2
EPS = 1e-6


class Tok:
    __slots__ = ("name", "last_w", "readers")

    def __init__(self, name=""):
        self.name = name
        self.last_w = None
        self.readers = []


class _Op:
    __slots__ = ("eng", "fn", "reads", "writes", "dma", "chan", "chan_idx", "waits", "signal", "sig_no", "inc", "barrier", "slot")

    def __init__(self, eng, fn, reads, writes, dma, chan, inc=16):
        self.eng, self.fn, self.reads, self.writes, self.dma, self.chan = eng, fn, reads, writes, dma, chan
        self.inc = inc
        self.barrier = False
        self.slot = -1
        self.chan_idx = 0
        self.waits = []
        self.signal = False
        self.sig_no = 0


class Prog:
    EPOCH = 30000
    COMPUTE = ("pe", "act", "dve", "pool")

    def __init__(self, nc):
        self.nc = nc
        self.ops = []
        self.chans = {}
        self.live = []

    def tok(self, name=""):
        t = Tok(name)
        self.live.append(t)
        return t

    def toks(self, n, name=""):
        return [self.tok(f"{name}{i}") for i in range(n)]

    def barrier(self):
        g = Tok("bar")
        o = _Op("sp", lambda e: e.nop(), [], list(self.live) + [g], False, None)
        o.barrier = True
        self.ops.append(o)
        for en in ("pe", "act", "dve", "pool"):
            self.ops.append(_Op(en, lambda e: e.nop(), [g], [], False, None))
        self.live = []

    def op(self, eng, fn, reads=(), writes=()):
        self.ops.append(_Op(eng, fn, list(reads), list(writes), False, None))

    def dma(self, eng, out, in_, reads=(), writes=(), chan=None):
        if chan is None:
            chan = writes[0] if writes else reads[0]
        o = _Op(eng, lambda e: e.dma_start(out=out, in_=in_), list(reads), list(writes), True, chan)
        self.ops.append(o)

    def collective(self, kind, src, dst, groups, reads=(), writes=()):
        tk = self.tok("cc")
        o = _Op("pool", lambda e: e.collective_compute(kind, ALU.bypass, replica_groups=groups, ins=[src], outs=[dst]),
                list(reads), list(writes) + [tk], True, tk, inc=1)
        self.ops.append(o)

    def finalize(self):
        ops = self.ops
        engs = ("pe", "act", "dve", "pool", "sp")
        waited = {e: {} for e in engs}
        active, free_slots, slot_total = {}, [], []
        for i, op in enumerate(ops):
            strong, weak = set(), set()
            for t in op.reads:
                if t.last_w is not None:
                    strong.add(t.last_w)
            for t in op.writes:
                if t.last_w is not None:
                    strong.add(t.last_w)
                weak.update(t.readers)
            need = {}
            for j in strong | weak:
                if j == i:
                    continue
                src = ops[j]
                if src.dma:
                    key, val = ("c", src.slot), src.chan_idx
                else:
                    if src.eng == op.eng and not op.dma:
                        if op.eng == "pe":
                            continue
                        if j not in strong:
                            continue
                    key, val = ("e", src.eng), j
                if val > need.get(key, (-1, None))[0]:
                    need[key] = (val, src)
            w = waited[op.eng]
            for key, (val, src) in need.items():
                if w.get(key, -1) >= val:
                    continue
                w[key] = val
                op.waits.append((key, val, src))
                if not src.dma:
                    src.signal = True
            if op.dma:
                c = id(op.chan)
                if c not in active:
                    active[c] = free_slots.pop() if free_slots else len(slot_total)
                    if active[c] == len(slot_total):
                        slot_total.append(0)
                op.slot = active[c]
                slot_total[op.slot] += op.inc
                op.chan_idx = slot_total[op.slot]
            if op.barrier:
                free_slots.extend(active.values())
                active.clear()
            for t in op.reads:
                t.readers.append(i)
            for t in op.writes:
                t.last_w = i
                t.readers = []
        cnt = {e: 0 for e in engs}
        for op in ops:
            if op.signal:
                cnt[op.eng] += 1
                op.sig_no = cnt[op.eng]
        nc = self.nc
        self.esems = {e: [nc.alloc_semaphore(name=f"s_{e}_{k}") for k in range((cnt[e] + self.EPOCH - 1) // self.EPOCH)]
                      for e in engs}
        self.csems = {}
        for c, n in enumerate(slot_total):
            assert n < 60000, f"too many DMAs on one channel semaphore: {n}"
            self.csems[c] = nc.alloc_semaphore(name=f"c_{c}")
        self.stats = dict(n_ops=len(ops), signals=dict(cnt), chans=len(slot_total), max_chan=max(slot_total or [0]),
                          per_eng={e: sum(1 for o in ops if o.eng == e) for e in engs})

    def emit(self, final_waits=()):
        nc = self.nc
        ops = self.ops
        E = self.EPOCH

        def run(engname, eng):
            for op in ops:
                if op.eng != engname:
                    continue
                for key, val, src in op.waits:
                    if key[0] == "c":
                        eng.wait_ge(self.csems[key[1]], val)
                    else:
                        s = src.sig_no - 1
                        eng.wait_ge(self.esems[src.eng][s // E], s % E + 1)
                ins = op.fn(eng)
                if op.dma:
                    ins.then_inc(self.csems[op.slot], op.inc)
                elif op.signal:
                    s = op.sig_no - 1
                    ins.then_inc(self.esems[op.eng][s // E], 1)

        with nc.Block() as block:
            @block.tensor
            def _(e):
                run("pe", e)

            @block.scalar
            def _(e):
                run("act", e)

            @block.vector
            def _(e):
                run("dve", e)

            @block.gpsimd
            def _(e):
                run("pool", e)

            @block.sync
            def _(e):
                run("sp", e)


class Ctx:
    def __init__(self, nc, P, bind=None, tag=""):
        import contextlib
        self.nc, self.P, self.bind, self.tag = nc, P, dict(bind or {}), tag
        self.stack = contextlib.ExitStack()

    def din(self, name, shape, dtype):
        if name in self.bind:
            ap = self.bind[name]
            assert tuple(ap.shape) == tuple(shape), (name, ap.shape, shape)
            return ap
        return self.nc.dram_tensor(name, list(shape), dtype, kind="ExternalInput").ap()

    def dout(self, name, shape, dtype):
        if name in self.bind:
            ap = self.bind[name]
            assert tuple(ap.shape) == tuple(shape), (name, ap.shape, shape)
            return ap
        return self.nc.dram_tensor(name, list(shape), dtype, kind="ExternalOutput").ap()

    def sb(self, name, shape, dt):
        return self.stack.enter_context(self.nc.sbuf_tensor(f"s_{self.tag}{name}", list(shape), dt))

    def ps(self, name, shape, dt):
        return self.stack.enter_context(self.nc.psum_tensor(f"p_{self.tag}{name}", list(shape), dt))

    def end(self):
        self.P.barrier()
        self.stack.close()


def _standalone(emit, *args, **kw):
    nc = bass.Bass("TRN2", target_bir_lowering=False)
    P = Prog(nc)
    ctx = Ctx(nc, P)
    emit(ctx, *args, **kw)
    P.finalize()
    P.emit()
    return nc, P


class Ring:
    def __init__(self, items):
        self.items = items
        self.i = 0

    def next(self):
        it = self.items[self.i % len(self.items)]
        self.i += 1
        return it


def _final_wait(P, toks):
    nc = P.nc
    P.op("sp", lambda e: e.nop(), reads=list(toks))


def emit_cast(ctx, NI):
    nc, P = ctx.nc, ctx.P
    src = ctx.din("src", [128, NI, 4096], F32)
    dst = ctx.dout("dst", [128, NI, 4096], BF16)
    CH = 2048
    nb = 3
    ins = [(ctx.sb(f"ci{i}", [128, CH], F32), P.tok()) for i in range(nb)]
    outs = [(ctx.sb(f"co{i}", [128, CH], BF16), P.tok()) for i in range(nb)]
    engs = ["dve", "pool", "act"]
    c = 0
    for i in range(NI):
        for hh in range(4096 // CH):
            it, itok = ins[c % nb]
            ot, otok = outs[c % nb]
            P.dma("sp", it[:], src[:, i, hh * CH:(hh + 1) * CH], writes=[itok])
            en = engs[c % 3]
            if en == "act":
                P.op("act", lambda e, ot=ot, it=it: e.activation(out=ot[:], in_=it[:], func=AF.Copy), reads=[itok], writes=[otok])
            else:
                P.op(en, lambda e, ot=ot, it=it: e.tensor_copy(out=ot[:], in_=it[:]), reads=[itok], writes=[otok])
            P.dma("pool", dst[:, i, hh * CH:(hh + 1) * CH], ot[:], reads=[otok], writes=[], chan=otok)
            c += 1
    ctx.end()


def emit_C(ctx, T, TS=512):
    nc, P = ctx.nc, ctx.P
    KC = 16
    xT_d = ctx.din("xT", [128, KC, T], F32)
    yT_d = ctx.din("yT", [128, KC, T], BF16)
    wo_d = ctx.din("wout_b", [16, 128, KC * 128], BF16)
    w1_d = ctx.din("w1_b", [64, 128, KC * 128], BF16)
    w2_d = ctx.din("w2_b", [64, 128, KC * 128], BF16)
    g_d = ctx.din("gff", [128, KC], F32)
    xo_d = ctx.dout("xoT", [128, KC, T], F32)
    NS = T // TS

    def sb(name, shape, dt):
        return ctx.sb(name, shape, dt)

    xs = [(sb(f"xs{i}", [128, KC, TS], F32), P.toks(KC)) for i in range(2)]
    ys = [(sb(f"ys{i}", [128, KC, TS], BF16), P.tok()) for i in range(2)]
    hT = (sb("hT", [128, KC, TS], BF16), P.toks(KC))
    u2 = [(sb(f"u2{i}", [128, 16, TS], BF16), P.toks(16)) for i in range(2)]
    NW = 6
    wb = Ring([(sb(f"wb{i}", [128, KC * 128], BF16), P.tok()) for i in range(NW)])
    sqr = Ring([(sb(f"sq{i}", [128, TS], BF16), P.tok()) for i in range(3)])
    rl = Ring([(sb(f"rl{i}", [128, TS], F32), P.tok()) for i in range(3)])
    rstd = (sb("rstd", [128, TS], F32), P.tok())
    ones = (sb("ones", [128, 128], BF16), P.tok())
    gff = (sb("gffs", [128, KC], F32), P.tok())
    pss = Ring([(ctx.ps(f"ps{i}", [128, TS], F32), P.tok()) for i in range(6)])
    ps_ss = (ctx.ps("ps_ss", [128, TS], F32), P.tok())
    out_tok = P.tok("xo")

    P.op("pool", lambda e: e.memset(ones[0][:], 1.0 / D_MODEL), writes=[ones[1]])
    P.dma("sp", gff[0][:], g_d, writes=[gff[1]])

    wlist = []
    for s in range(NS):
        wlist += [wo_d[cc] for cc in range(16)]
        seq = ["f1q0", "f1q1", "f2q0", "f1q2", "f2q1", "f1q3", "f2q2", "f2q3"]
        for st in seq:
            q = int(st[3])
            if st.startswith("f1"):
                wlist += [w1_d[q * 16 + j] for j in range(16)]
            else:
                wlist += [w2_d[q * 16 + cc] for cc in range(16)]
    wstate = {"issued": 0, "used": 0, "cur": {}}

    def w_prefetch(upto):
        while wstate["issued"] < min(upto, len(wlist)):
            i = wstate["issued"]
            t, tk = wb.next()
            P.dma("sp", t[:], wlist[i], writes=[tk])
            wstate["cur"][i] = (t, tk)
            wstate["issued"] += 1

    def w_get():
        i = wstate["used"]
        w_prefetch(i + NW - 1)
        wstate["used"] += 1
        w_prefetch(i + NW)
        return wstate["cur"].pop(i)

    def load_xy(s):
        xt, xtk = xs[s % 2]
        yt, ytk = ys[s % 2]
        for h in range(2):
            P.dma("pool", xt[:, h * 8:(h + 1) * 8, :], xT_d[:, h * 8:(h + 1) * 8, s * TS:(s + 1) * TS],
                  writes=xtk[h * 8:(h + 1) * 8], chan=xtk[h * 8])
        P.dma("pool", yt[:], yT_d[:, :, s * TS:(s + 1) * TS], writes=[ytk])

    load_xy(0)
    for s in range(NS):
        xt, xtk = xs[s % 2]
        yt, ytk = ys[s % 2]
        if s + 1 < NS:
            load_xy(s + 1)
        for cc in range(16):
            wt, wtk = w_get()
            pt, ptk = pss.next()
            for kc in range(KC):
                P.op("pe", lambda e, pt=pt, wt=wt, yt=yt, kc=kc: e.matmul(
                    pt[:], wt[:, kc * 128:(kc + 1) * 128], yt[:, kc, :], start=(kc == 0), stop=(kc == KC - 1)),
                    reads=[wtk, ytk], writes=[ptk])
            P.op("dve", lambda e, xt=xt, pt=pt, cc=cc: e.tensor_tensor(
                out=xt[:, cc, :], in0=xt[:, cc, :], in1=pt[:], op=ALU.add), reads=[ptk, xtk[cc]], writes=[xtk[cc]])
            st_, stk = sqr.next()
            P.op("act", lambda e, st_=st_, xt=xt, cc=cc: e.activation(out=st_[:], in_=xt[:, cc, :], func=AF.Square),
                 reads=[xtk[cc]], writes=[stk])
            P.op("pe", lambda e, st_=st_, cc=cc: e.matmul(ps_ss[0][:], ones[0][:], st_[:], start=(cc == 0), stop=(cc == 15)),
                 reads=[stk, ones[1]], writes=[ps_ss[1]])
        P.op("dve", lambda e: e.tensor_scalar_add(out=rstd[0][:], in0=ps_ss[0][:], scalar1=EPS),
             reads=[ps_ss[1]], writes=[rstd[1]])
        P.op("act", lambda e: e.activation(out=rstd[0][:], in_=rstd[0][:], func=AF.Sqrt),
             reads=[rstd[1]], writes=[rstd[1]])
        P.op("dve", lambda e: e.reciprocal(out=rstd[0][:], in_=rstd[0][:]),
             reads=[rstd[1]], writes=[rstd[1]])
        for kc in range(KC):
            P.op("dve", lambda e, xt=xt, kc=kc: e.scalar_tensor_tensor(
                out=hT[0][:, kc, :], in0=xt[:, kc, :], scalar=gff[0][:, kc:kc + 1], in1=rstd[0][:],
                op0=ALU.mult, op1=ALU.mult), reads=[xtk[kc], gff[1], rstd[1]], writes=[hT[1][kc]])

        def f1(q):
            ut, utk = u2[q % 2]
            for j in range(16):
                wt, wtk = w_get()
                pt, ptk = pss.next()
                for kc in range(KC):
                    P.op("pe", lambda e, pt=pt, wt=wt, kc=kc: e.matmul(
                        pt[:], wt[:, kc * 128:(kc + 1) * 128], hT[0][:, kc, :], start=(kc == 0), stop=(kc == KC - 1)),
                        reads=[wtk, hT[1][kc]], writes=[ptk])
                rt, rtk = rl.next()
                P.op("act", lambda e, rt=rt, pt=pt: e.activation(out=rt[:], in_=pt[:], func=AF.Relu),
                     reads=[ptk], writes=[rtk])
                P.op("pool", lambda e, ut=ut, rt=rt, j=j: e.tensor_tensor(out=ut[:, j, :], in0=rt[:], in1=rt[:], op=ALU.mult),
                     reads=[rtk], writes=[utk[j]])

        def f2(q):
            ut, utk = u2[q % 2]
            for cc in range(16):
                wt, wtk = w_get()
                pt, ptk = pss.next()
                for kc in range(16):
                    P.op("pe", lambda e, pt=pt, wt=wt, ut=ut, kc=kc: e.matmul(
                        pt[:], wt[:, kc * 128:(kc + 1) * 128], ut[:, kc, :], start=(kc == 0), stop=(kc == 15)),
                        reads=[wtk, utk[kc]], writes=[ptk])
                P.op("dve", lambda e, xt=xt, pt=pt, cc=cc: e.tensor_tensor(
                    out=xt[:, cc, :], in0=xt[:, cc, :], in1=pt[:], op=ALU.add), reads=[ptk, xtk[cc]], writes=[xtk[cc]])
                if q == 3:
                    P.dma("pool", xo_d[:, cc, s * TS:(s + 1) * TS], xt[:, cc, :], reads=[xtk[cc]], writes=[out_tok],
                          chan=xtk[cc])

        f1(0); f1(1); f2(0); f1(2); f2(1); f1(3); f2(2); f2(3)
    ctx.end()


C_MLQ, C_MLK, C_MLV, C_MLO, C_MLG, C_CQ, C_CKV, C_DQ, C_DK, C_DV = 0, 256, 512, 1024, 1536, 1552, 1936, 2128, 2896, 3664
D_IN = 4432
TOK_GROUPS = [("mlk", 256, 256), ("mlv", 512, 512), ("mlo", 1024, 512), ("gq", 1536, 400), ("ckv", 1936, 192),
              ("dq0", 2128, 384), ("dq1", 2512, 384), ("dk0", 2896, 384), ("dk1", 3280, 384),
              ("dv0", 3664, 384), ("dv1", 4048, 384)]
FM_CHUNKS = [("mlq0", 0), ("mlq1", 128), ("mlk0", 256), ("mlk1", 384), ("cq0", 1552), ("cq1", 1680), ("cq2", 1808),
             ("ckv", 1936)]
MLA_SCALE = 192 ** -0.5
DIL_SCALE = 128 ** -0.5


def emit_A(ctx, T, TS=512):
    nc, P = ctx.nc, ctx.P
    KC = 16
    NS = T // TS
    dt_ = nc.dram_tensor
    xT_d = ctx.din("xT", [128, KC, T], F32)
    win_d = ctx.din("win", [128, KC, D_IN], BF16)
    gmix_d = ctx.din("gmix", [128, KC], F32)
    wq_d = ctx.din("wq", [128, 3, 1152], F32)
    wkv_d = ctx.din("wkv", [128, 1536], F32)
    qlat_d = ctx.din("qlat_g", [128, 3], F32)
    kvlat_d = ctx.din("kvlat_g", [128, 1], F32)
    gains_d = ctx.din("gains", [128, 704], F32)
    rope_d = ctx.din("rope", [T, 96], F32)
    ident_d = ctx.din("ident", [128, 128], BF16)
    o_qTml = ctx.dout("o_qTml", [4, 64, T], BF16)
    o_kTml = ctx.dout("o_kTml", [4, 64, T], BF16)
    NJc = T // 128
    o_kb = ctx.dout("o_kb", [4, 128, NJc, 64], BF16)
    o_vb = ctx.dout("o_vb", [4, 128, NJc, 128], BF16)
    o_osb = ctx.dout("o_osb", [4, 128, NJc, 128], F32)
    o_gb = ctx.dout("o_gb", [128, NJc, 16], F32)
    o_QTn = ctx.dout("o_QTn", [6, 128, T], BF16)
    o_QTr = ctx.dout("o_QTr", [6, 64, T], BF16)
    o_KTn = ctx.dout("o_KTn", [6, 128, T], BF16)
    o_KTr = ctx.dout("o_KTr", [64, T], BF16)
    o_Vb = ctx.dout("o_Vb", [6, 128, T], BF16)
    o_QTd = ctx.dout("o_QTd", [6, 128, T], BF16)
    o_KTd = ctx.dout("o_KTd", [6, 128, T], BF16)
    o_Vdb = ctx.dout("o_Vdb", [6, 128, T], BF16)
    cnt = [0]

    def sb(shape, dt, name=None):
        cnt[0] += 1
        return ctx.sb((name or f"a{cnt[0]}"), shape, dt)

    def ring(n, shape, dt, name):
        return Ring([(sb(shape, dt, f"{name}{i}"), P.tok(f"{name}{i}")) for i in range(n)])

    def psring(n, shape, dt, name):
        return Ring([(ctx.ps(f"{name}{i}", shape, dt), P.tok(f"{name}{i}")) for i in range(n)])

    outs = []

    xs = (sb([128, KC, TS], F32, "xs"), P.toks(KC))
    hT = (sb([128, KC, TS], BF16, "hT"), P.toks(KC))
    wtok = ring(2, [128, KC, 512], BF16, "wtok")
    wfm = ring(3, [128, KC, 128], BF16, "wfm")
    sqr = ring(3, [128, TS], BF16, "sq")
    rstd = (sb([128, TS], F32, "rstd"), P.tok())
    ones = (sb([128, 128], BF16, "ones"), P.tok())
    ident = (sb([128, 128], BF16, "ident"), P.tok())
    gmix = (sb([128, KC], F32, "gmix"), P.tok())
    wq = (sb([128, 3, 1152], BF16, "wq"), P.tok())
    wkv = (sb([128, 1536], BF16, "wkv"), P.tok())
    wtmp = (sb([128, 3, 1152], F32, "wtmp"), P.tok())
    qlat = (sb([128, 3], F32, "qlat"), P.tok())
    kvlat = (sb([128, 1], F32, "kvlat"), P.tok())
    gains = (sb([128, 704], F32, "gains"), P.tok())
    cqT = [(sb([128, TS], BF16, f"cqT{i}"), P.tok()) for i in range(3)]
    ckvT = (sb([128, TS], BF16, "ckvT"), P.tok())
    fmout = ring(3, [128, TS], BF16, "fmout")
    rope_t = ring(4, [128, 96], F32, "rope")
    qfx4 = ring(4, [128, 384], F32, "qfx")
    qbq4 = ring(4, [128, 1152], BF16, "qbq")
    stats = ring(8, [128, 16], F32, "stats")
    sqs = ring(2, [128, 1152], F32, "sqs")
    fx = ring(2, [128, 1152], F32, "fx")
    rtmp = ring(3, [128, 6, 32], F32, "rtmp")
    qb = ring(3, [128, 1152], BF16, "qb")
    ob32 = ring(2, [128, 512], F32, "ob32")
    tst = ring(4, [128, 4, 128], BF16, "tst")
    ps_ss = (ctx.ps("ps_ss", [128, TS], F32), P.tok())
    pg = psring(3, [128, 512], F32, "pg")
    pr = psring(2, [128, 512], F32, "pr")
    ptr = psring(2, [128, 4, 128], BF16, "ptr")

    P.op("pool", lambda e: e.memset(ones[0][:], 1.0 / D_MODEL), writes=[ones[1]])
    P.dma("sp", ident[0][:], ident_d, writes=[ident[1]])
    P.dma("sp", gmix[0][:], gmix_d, writes=[gmix[1]])
    P.dma("sp", qlat[0][:], qlat_d, writes=[qlat[1]])
    P.dma("sp", kvlat[0][:], kvlat_d, writes=[kvlat[1]])
    P.dma("sp", gains[0][:], gains_d, writes=[gains[1]])
    P.dma("sp", wtmp[0][:], wq_d, writes=[wtmp[1]])
    for kc in range(3):
        P.op("dve", lambda e, kc=kc: e.tensor_scalar_mul(out=wq[0][:, kc, :], in0=wtmp[0][:, kc, :], scalar1=qlat[0][:, kc:kc + 1]),
             reads=[wtmp[1], qlat[1]], writes=[wq[1]])
    wt2 = wtmp[0][:].rearrange("p a b -> p (a b)")
    P.dma("sp", wt2[:, 0:1536], wkv_d, writes=[wtmp[1]])
    P.op("dve", lambda e: e.tensor_scalar_mul(out=wkv[0][:], in0=wt2[:, 0:1536], scalar1=kvlat[0][:, 0:1]),
         reads=[wtmp[1], kvlat[1]], writes=[wkv[1]])

    G_QH, G_KH, G_DQ, G_DK = 0, 192, 384, 512

    def out_dma(dst, src, tk):
        P.dma("pool", dst, src, reads=[tk], writes=[], chan=tk)
        if tk not in outs:
            outs.append(tk)

    def rsqrt_inplace(t_ap, tk):
        P.op("act", lambda e: e.activation(out=t_ap, in_=t_ap, func=AF.Ln), reads=[tk], writes=[tk])
        P.op("act", lambda e: e.activation(out=t_ap, in_=t_ap, func=AF.Exp, scale=-0.5), reads=[tk], writes=[tk])

    def head_norm(src, srctk, H, dh, pre, gain_ap, scale, dst, dsttk, st, sttk, c0):
        sq, sqk = sqs.next()
        sqv = sq[:, 0:H * dh].rearrange("p (h d) -> p h d", h=H)
        P.op("act", lambda e: e.activation(out=sqv, in_=src, func=AF.Square, scale=float(dh) ** -0.5),
             reads=[srctk], writes=[sqk])
        ms = st[:, c0:c0 + H]
        P.op("dve", lambda e: e.tensor_reduce(out=ms, in_=sqv, axis=AX.X, op=ALU.add), reads=[sqk], writes=[sttk])
        if pre is not None:
            pap, ptk = pre
            P.op("dve", lambda e: e.tensor_scalar(out=ms, in0=ms, scalar1=pap, scalar2=pap, op0=ALU.mult, op1=ALU.mult),
                 reads=[sttk, ptk], writes=[sttk])
        P.op("dve", lambda e: e.tensor_scalar_add(out=ms, in0=ms, scalar1=EPS), reads=[sttk], writes=[sttk])
        rsqrt_inplace(ms, sttk)
        if pre is not None:
            P.op("dve", lambda e: e.tensor_scalar(out=ms, in0=ms, scalar1=pap, scalar2=float(scale), op0=ALU.mult, op1=ALU.mult),
                 reads=[sttk, ptk], writes=[sttk])
        elif scale != 1.0:
            P.op("dve", lambda e: e.tensor_scalar_mul(out=ms, in0=ms, scalar1=float(scale)), reads=[sttk], writes=[sttk])
        P.op("dve", lambda e: e.tensor_tensor(out=dst, in0=src, in1=ms.unsqueeze(2).to_broadcast([128, H, dh]), op=ALU.mult),
             reads=[srctk, sttk], writes=[dsttk])
        P.op("pool", lambda e: e.tensor_tensor(out=dst, in0=dst, in1=gain_ap.unsqueeze(1).to_broadcast([128, H, dh]), op=ALU.mult),
             reads=[dsttk, gains[1]], writes=[dsttk])

    def rope(x1, x2, H, hd, cos, sin, ropetk, srctk, o1, o2, otk):
        cb = cos.unsqueeze(1).to_broadcast([128, H, hd])
        sn = sin.unsqueeze(1).to_broadcast([128, H, hd])
        ta, tak = rtmp.next()
        tb, tbk = rtmp.next()
        a = ta[:, 0:H, 0:hd]
        b = tb[:, 0:H, 0:hd]
        P.op("dve", lambda e: e.tensor_tensor(out=a, in0=x2, in1=sn, op=ALU.mult), reads=[srctk, ropetk], writes=[tak])
        P.op("pool", lambda e: e.tensor_tensor(out=b, in0=x1, in1=sn, op=ALU.mult), reads=[srctk, ropetk], writes=[tbk])
        tc_, tck = rtmp.next()
        c = tc_[:, 0:H, 0:hd]
        P.op("dve", lambda e: e.tensor_tensor(out=c, in0=x1, in1=cb, op=ALU.mult), reads=[srctk, ropetk], writes=[tck])
        P.op("dve", lambda e: e.tensor_tensor(out=o1, in0=c, in1=a, op=ALU.subtract), reads=[tck, tak], writes=[otk])
        td, tdk = rtmp.next()
        dd = td[:, 0:H, 0:hd]
        P.op("pool", lambda e: e.tensor_tensor(out=dd, in0=x2, in1=cb, op=ALU.mult), reads=[srctk, ropetk], writes=[tdk])
        P.op("pool", lambda e: e.tensor_tensor(out=o2, in0=dd, in1=b, op=ALU.add), reads=[tdk, tbk], writes=[otk])

    def transpose_out(src_fn, srctk, nh, rows, dst_fn):
        h0 = 0
        while h0 < nh:
            n = min(4, nh - h0)
            pt, ptk = ptr.next()
            for i in range(n):
                P.op("pe", lambda e, pt=pt, i=i, h=h0 + i: e.transpose(pt[0:rows, i, :], src_fn(h), ident[0][:]),
                     reads=[srctk, ident[1]], writes=[ptk])
            stt, sttk = tst.next()
            P.op("act", lambda e, pt=pt, stt=stt, n=n: e.activation(out=stt[0:rows, 0:n, :], in_=pt[0:rows, 0:n, :], func=AF.Copy),
                 reads=[ptk], writes=[sttk])
            for i in range(n):
                out_dma(dst_fn(h0 + i), stt[0:rows, i, :], sttk)
            h0 += n

    def load_wtok(g):
        name, c0, w = TOK_GROUPS[g]
        t, tk = wtok.next()
        P.dma("sp", t[:, :, 0:w], win_d[:, :, c0:c0 + w], writes=[tk])
        return t, tk

    def load_wfm(i):
        name, c0 = FM_CHUNKS[i]
        t, tk = wfm.next()
        P.dma("sp", t[:], win_d[:, :, c0:c0 + 128], writes=[tk])
        return t, tk

    for s in range(NS):
        t0 = s * TS
        for h in range(2):
            P.dma("pool", xs[0][:, h * 8:(h + 1) * 8, :], xT_d[:, h * 8:(h + 1) * 8, t0:t0 + TS],
                  writes=xs[1][h * 8:(h + 1) * 8], chan=xs[1][h * 8])
        for kc in range(KC):
            st_, stk = sqr.next()
            P.op("act", lambda e, st_=st_, kc=kc: e.activation(out=st_[:], in_=xs[0][:, kc, :], func=AF.Square),
                 reads=[xs[1][kc]], writes=[stk])
            P.op("pe", lambda e, st_=st_, kc=kc: e.matmul(ps_ss[0][:], ones[0][:], st_[:], start=(kc == 0), stop=(kc == KC - 1)),
                 reads=[stk, ones[1]], writes=[ps_ss[1]])
        P.op("dve", lambda e: e.tensor_scalar_add(out=rstd[0][:], in0=ps_ss[0][:], scalar1=EPS), reads=[ps_ss[1]], writes=[rstd[1]])
        rsqrt_inplace(rstd[0][:], rstd[1])
        for kc in range(KC):
            P.op("dve", lambda e, kc=kc: e.scalar_tensor_tensor(
                out=hT[0][:, kc, :], in0=xs[0][:, kc, :], scalar=gmix[0][:, kc:kc + 1], in1=rstd[0][:],
                op0=ALU.mult, op1=ALU.mult), reads=[xs[1][kc], gmix[1], rstd[1]], writes=[hT[1][kc]])
        for i, (name, c0) in enumerate(FM_CHUNKS):
            wt, wtk = load_wfm(i)
            pt, ptk = pg.next()
            for kc in range(KC):
                P.op("pe", lambda e, pt=pt, wt=wt, kc=kc: e.matmul(pt[:], wt[:, kc, :], hT[0][:, kc, :], start=(kc == 0), stop=(kc == KC - 1)),
                     reads=[wtk, hT[1][kc]], writes=[ptk])
            if name.startswith("ml"):
                ft, ftk = fmout.next()
                P.op("act", lambda e, ft=ft, pt=pt: e.activation(out=ft[:], in_=pt[:], func=AF.Copy), reads=[ptk], writes=[ftk])
                dsto = (o_qTml if name[2] == "q" else o_kTml)
                c2 = int(name[3])
                out_dma(dsto[2 * c2, :, t0:t0 + TS], ft[0:64, :], ftk)
                out_dma(dsto[2 * c2 + 1, :, t0:t0 + TS], ft[64:128, :], ftk)
            else:
                ft, ftk = cqT[int(name[2])] if name.startswith("cq") else ckvT
                P.op("act", lambda e, ft=ft, pt=pt: e.activation(out=ft[:], in_=pt[:], func=AF.Copy), reads=[ptk], writes=[ftk])
        per_tt = [dict() for _ in range(4)]
        for g, (name, c0, w) in enumerate(TOK_GROUPS):
            wt, wtk = load_wtok(g)
            for tt in range(4):
                r0 = t0 + tt * 128
                pt, ptk = pg.next()
                for kc in range(KC):
                    P.op("pe", lambda e, pt=pt, wt=wt, kc=kc, tt=tt, w=w: e.matmul(
                        pt[:, 0:w], hT[0][:, kc, tt * 128:(tt + 1) * 128], wt[:, kc, 0:w], start=(kc == 0), stop=(kc == KC - 1)),
                        reads=[wtk, hT[1][kc]], writes=[ptk])
                if name in ("mlk", "mlv", "dv0", "dv1"):
                    bt, btk = qb.next()
                    P.op("act", lambda e, bt=bt, pt=pt, w=w: e.activation(out=bt[:, 0:w], in_=pt[:, 0:w], func=AF.Copy),
                         reads=[ptk], writes=[btk])
                    jj = r0 // 128
                    if name == "mlk":
                        dst = o_kb[:, :, jj, :].rearrange("h p d -> p h d")
                        srcv = bt[:, 0:256].rearrange("p (h d) -> p h d", h=4)
                    elif name == "mlv":
                        dst = o_vb[:, :, jj, :].rearrange("h p d -> p h d")
                        srcv = bt[:, 0:512].rearrange("p (h d) -> p h d", h=4)
                    else:
                        k = int(name[2])
                        dst = o_Vdb[3 * k:3 * k + 3, :, jj * 128:(jj + 1) * 128].rearrange("h p d -> p h d")
                        srcv = bt[:, 0:384].rearrange("p (h d) -> p h d", h=3)
                    out_dma(dst, srcv, btk)
                elif name == "mlo":
                    ot, otk = ob32.next()
                    P.op("act", lambda e, ot=ot, pt=pt: e.activation(out=ot[:], in_=pt[:], func=AF.Exp, scale=-1.0),
                         reads=[ptk], writes=[otk])
                    P.op("dve", lambda e, ot=ot: e.tensor_scalar_add(out=ot[:], in0=ot[:], scalar1=1.0), reads=[otk], writes=[otk])
                    P.op("dve", lambda e, ot=ot: e.reciprocal(out=ot[:], in_=ot[:]), reads=[otk], writes=[otk])
                    out_dma(o_osb[:, :, r0 // 128, :].rearrange("h p d -> p h d"), ot[:].rearrange("p (h d) -> p h d", h=4), otk)
                elif name == "gq":
                    ot, otk = ob32.next()
                    P.op("dve", lambda e, ot=ot, pt=pt: e.tensor_copy(out=ot[:, 0:16], in_=pt[:, 0:16]), reads=[ptk], writes=[otk])
                    out_dma(o_gb[:, r0 // 128, :], ot[:, 0:16], otk)
                    st, stk = stats.next()
                    sq, sqk = sqs.next()
                    P.op("act", lambda e, sq=sq, pt=pt, st=st: e.activation(out=sq[:, 0:384], in_=pt[:, 16:400], func=AF.Square,
                                                                             scale=384.0 ** -0.5, accum_out=st[:, 15:16]),
                         reads=[ptk], writes=[sqk, stk])
                    P.op("dve", lambda e, st=st: e.tensor_scalar_add(out=st[:, 15:16], in0=st[:, 15:16], scalar1=EPS), reads=[stk], writes=[stk])
                    rsqrt_inplace(st[:, 15:16], stk)
                    pre = (st[:, 15:16], stk)
                    fxt, fxk = qfx4.next()
                    fxv = fxt[:].rearrange("p (h d) -> p h d", h=6)
                    qbt, qbk = qbq4.next()
                    qbv = qbt[:].rearrange("p (h d) -> p h d", h=6)
                    for b in range(3):
                        pq, pqk = pr.next()
                        for kc in range(3):
                            P.op("pe", lambda e, pq=pq, kc=kc, b=b, tt=tt: e.matmul(
                                pq[:, 0:384], cqT[kc][0][:, tt * 128:(tt + 1) * 128], wq[0][:, kc, b * 384:(b + 1) * 384],
                                start=(kc == 0), stop=(kc == 2)), reads=[cqT[kc][1], wq[1]], writes=[pqk])
                        pv = pq[:, 0:384].rearrange("p (h d) -> p h d", h=2)
                        head_norm(pv[:, :, 0:128], pqk, 2, 128, pre, gains[0][:, G_QH:G_QH + 128], MLA_SCALE,
                                  qbv[:, 2 * b:2 * b + 2, 0:128], qbk, st, stk, 2 * b)
                        head_norm(pv[:, :, 128:192], pqk, 2, 64, pre, gains[0][:, G_QH + 128:G_QH + 192], MLA_SCALE,
                                  fxv[:, 2 * b:2 * b + 2, :], fxk, st, stk, 6 + 2 * b)
                    per_tt[tt]["q"] = (fxv, fxk, qbt, qbv, qbk)
                elif name == "ckv":
                    rt, rtk = rope_t.next()
                    P.dma("sp", rt[:], rope_d[r0:r0 + 128, :], writes=[rtk])
                    per_tt[tt]["rope"] = (rt, rtk)
                    fxv, fxk, qbt, qbv, qbk = per_tt[tt]["q"]
                    rope(fxv[:, :, 0:32], fxv[:, :, 32:64], 6, 32, rt[:, 0:32], rt[:, 32:64], rtk, fxk,
                         qbv[:, :, 128:160], qbv[:, :, 160:192], qbk)
                    transpose_out(lambda h, qbv=qbv: qbv[:, h, 0:128], qbk, 6, 128, lambda h: o_QTn[h, :, r0:r0 + 128])
                    transpose_out(lambda h, qbv=qbv: qbv[:, h, 128:192], qbk, 6, 64, lambda h: o_QTr[h, :, r0:r0 + 128])
                    st, stk = stats.next()
                    sq, sqk = sqs.next()
                    P.op("act", lambda e, sq=sq, pt=pt, st=st: e.activation(out=sq[:, 0:128], in_=pt[:, 0:128], func=AF.Square,
                                                                             scale=128.0 ** -0.5, accum_out=st[:, 15:16]),
                         reads=[ptk], writes=[sqk, stk])
                    P.op("dve", lambda e, st=st: e.tensor_scalar_add(out=st[:, 15:16], in0=st[:, 15:16], scalar1=EPS), reads=[stk], writes=[stk])
                    rsqrt_inplace(st[:, 15:16], stk)
                    pre = (st[:, 15:16], stk)
                    fxt, fxk = fx.next()
                    kbt, kbk = qb.next()
                    kr = fxt[:, 0:64].rearrange("p (h d) -> p h d", h=1)
                    head_norm(pt[:, 128:192].rearrange("p (h d) -> p h d", h=1), ptk, 1, 64, None, gains[0][:, G_KH + 128:G_KH + 192], 1.0,
                              kr, fxk, st, stk, 12)
                    kro = kbt[:, 768:832].rearrange("p (h d) -> p h d", h=1)
                    rope(kr[:, :, 0:32], kr[:, :, 32:64], 1, 32, rt[:, 0:32], rt[:, 32:64], rtk, fxk, kro[:, :, 0:32], kro[:, :, 32:64], kbk)
                    kbv = kbt[:, 0:768].rearrange("p (h d) -> p h d", h=6)
                    vt, vtk = qb.next()
                    vv = vt[:, 0:768].rearrange("p (h d) -> p h d", h=6)
                    for b in range(3):
                        pk, pkk = pr.next()
                        P.op("pe", lambda e, pk=pk, b=b, tt=tt: e.matmul(pk[:], ckvT[0][:, tt * 128:(tt + 1) * 128], wkv[0][:, b * 512:(b + 1) * 512],
                                                                           start=True, stop=True), reads=[ckvT[1], wkv[1]], writes=[pkk])
                        pv = pk[:].rearrange("p (h d) -> p h d", h=2)
                        head_norm(pv[:, :, 0:128], pkk, 2, 128, pre, gains[0][:, G_KH:G_KH + 128], 1.0,
                                  kbv[:, 2 * b:2 * b + 2, :], kbk, st, stk, 2 * b)
                        P.op("dve", lambda e, pv=pv, b=b, vv=vv, st=st: e.tensor_scalar_mul(out=vv[:, 2 * b:2 * b + 2, :], in0=pv[:, :, 128:256],
                                                                                            scalar1=st[:, 15:16]), reads=[pkk, stk], writes=[vtk])
                    out_dma(o_Vb[:, :, r0:r0 + 128].rearrange("h p d -> p h d"), vt[:, 0:768].rearrange("p (h d) -> p h d", h=6), vtk)
                    transpose_out(lambda h, kbv=kbv: kbv[:, h, :], kbk, 6, 128, lambda h: o_KTn[h, :, r0:r0 + 128])
                    transpose_out(lambda h, kbt=kbt: kbt[:, 768:832], kbk, 1, 64, lambda h: o_KTr[:, r0:r0 + 128])
                elif name in ("dq0", "dq1", "dk0", "dk1"):
                    isq = name[1] == "q"
                    k = int(name[2])
                    rt, rtk = per_tt[tt]["rope"]
                    st, stk = stats.next()
                    fxt, fxk = fx.next()
                    fxv = fxt[:, 0:384].rearrange("p (h d) -> p h d", h=3)
                    dbt, dbk = qb.next()
                    dbv = dbt[:, 0:384].rearrange("p (h d) -> p h d", h=3)
                    goff = G_DQ if isq else G_DK
                    head_norm(pt[:, 0:384].rearrange("p (h d) -> p h d", h=3), ptk, 3, 128, None, gains[0][:, goff:goff + 128],
                              DIL_SCALE if isq else 1.0, fxv, fxk, st, stk, 0)
                    rope(fxv[:, :, 0:16], fxv[:, :, 16:32], 3, 16, rt[:, 64:80], rt[:, 80:96], rtk, fxk, dbv[:, :, 0:16], dbv[:, :, 16:32], dbk)
                    P.op("act", lambda e, dbv=dbv, fxv=fxv: e.activation(out=dbv[:, :, 32:128], in_=fxv[:, :, 32:128], func=AF.Copy),
                         reads=[fxk], writes=[dbk])
                    dst = o_QTd if isq else o_KTd
                    transpose_out(lambda h, dbv=dbv: dbv[:, h, :], dbk, 3, 128, lambda h, k=k, dst=dst: dst[3 * k + h, :, r0:r0 + 128])
    ctx.end()


DIL_HALO = 1024
DIL_NKT = 20


def dil_masks():
    i = np.arange(DIL_NKT)[:, None, None]
    p = np.arange(128)[None, :, None]
    c = np.arange(512)[None, None, :]
    dlt = 128 * i - DIL_HALO + p - c
    m = (np.abs(dlt) <= 64).astype(np.float32)
    m += ((np.abs(dlt) <= 256) & (dlt % 4 == 0))
    m += ((np.abs(dlt) <= 1024) & (dlt % 16 == 0))
    return m.astype(np.float32)


def emit_ATT(ctx, T, S, QT=512):
    nc, P = ctx.nc, ctx.P
    dt_ = nc.dram_tensor
    NKT = S // 128
    NQ = T // QT
    TH = T + 2 * DIL_HALO
    NHT = TH // 128
    QTn_d = ctx.din("QTn", [6, 128, T], BF16)
    QTr_d = ctx.din("QTr", [6, 64, T], BF16)
    assert S == 4 * T
    KTn_g = ctx.din("KTn_g", [4, 6, 128, T], BF16)
    KTr_g = ctx.din("KTr_g", [4, 64, T], BF16)
    Vb_g = ctx.din("Vb_g", [4, 6, 128, T], BF16)
    QTd_d = ctx.din("QTd", [6, 128, T], BF16)
    KTd_o = ctx.din("KTd_o", [6, 128, T], BF16)
    Vdb_o = ctx.din("Vdb_o", [6, 128, T], BF16)
    KTd_g = ctx.din("KTd_g", [4, 6, 128, T], BF16)
    Vdb_g = ctx.din("Vdb_g", [4, 6, 128, T], BF16)
    sel_d = ctx.din("sel", [128, 12], F32)
    valid_d = ctx.din("valid", [128, NHT], F32)
    mask_d = ctx.din("masks", [128, DIL_NKT, 512], BF16)
    y_mla = ctx.dout("y_mla", [6, 128, T], BF16)
    y_dil = ctx.dout("y_dil", [6, 128, T], BF16)
    cnt = [0]

    def sb(shape, dt, name):
        return ctx.sb(name, shape, dt)

    def ring(n, shape, dt, name):
        return Ring([(sb(shape, dt, f"{name}{i}"), P.tok(f"{name}{i}")) for i in range(n)])

    def psring(n, shape, dt, name):
        return Ring([(ctx.ps(f"{name}{i}", shape, dt), P.tok(f"{name}{i}")) for i in range(n)])

    kbuf = ring(2, [128, S], BF16, "kbuf")
    vbuf = ring(2, [128, NKT * 128], BF16, "vbuf")
    ktr = (sb([64, S], BF16, "ktr"), P.tok())
    qn = ring(2, [128, T], BF16, "qn")
    qr = ring(2, [64, T], BF16, "qr")
    pT = ring(5, [128, QT], BF16, "pT")
    eT = ring(2, [128, QT], BF16, "eT")
    ones = (sb([128, 128], BF16, "ones"), P.tok())
    rden = ring(1, [128, QT], F32, "rden")
    ost = ring(1, [128, QT], BF16, "ost")
    MW = DIL_NKT * 512
    alias_masks = (S - TH) >= MW
    if alias_masks:
        masks_ap = kbuf.items[0][0][:, TH:TH + MW].rearrange("p (i c) -> p i c", i=DIL_NKT)
    else:
        masks_ap = sb([128, DIL_NKT, 512], BF16, "masks")[:]
    masks = (masks_ap, P.tok())
    valid = (sb([128, NHT], F32, "valid"), P.tok())
    sel = (sb([128, 12], F32, "sel"), P.tok())
    htmp = ring(1, [128, 4, DIL_HALO // 4], BF16, "htmp")
    ps_s = psring(4, [128, QT], F32, "ps_s")
    ps_o = psring(2, [128, QT], F32, "ps_o")
    ps_d = psring(2, [128, QT], F32, "ps_d")
    outs = []

    P.op("pool", lambda e: e.memset(ones[0][:], 1.0), writes=[ones[1]])
    for i in range(4):
        P.dma("sp", ktr[0][:, i * T:(i + 1) * T], KTr_g[i], writes=[ktr[1]])
    P.dma("sp", sel[0][:], sel_d, writes=[sel[1]])
    P.dma("sp", valid[0][:], valid_d, writes=[valid[1]])

    def finish(po, pok, pd, pdk, dst):
        rd, rdk = rden.next()
        P.op("dve", lambda e: e.reciprocal(out=rd[:], in_=pd[:]), reads=[pdk], writes=[rdk])
        ot, otk = ost.next()
        P.op("dve", lambda e: e.tensor_tensor(out=ot[:], in0=po[:], in1=rd[:], op=ALU.mult), reads=[pok, rdk], writes=[otk])
        P.dma("pool", dst, ot[:], reads=[otk], writes=[], chan=otk)
        if otk not in outs:
            outs.append(otk)

    def load_head(h):
        kt, ktk = kbuf.next()
        vt, vtk = vbuf.next()
        qt, qtk = qn.next()
        qrt, qrk = qr.next()
        P.dma("sp", qt[:], QTn_d[h], writes=[qtk])
        P.dma("sp", qrt[:], QTr_d[h], writes=[qrk])
        for i in range(4):
            P.dma("sp", kt[:, i * T:(i + 1) * T], KTn_g[i, h], writes=[ktk])
            P.dma("pool", vt[:, i * T:(i + 1) * T], Vb_g[i, h], writes=[vtk])
        return kt, ktk, vt, vtk, qt, qtk, qrt, qrk

    nxt = load_head(0)
    for h in range(6):
        kt, ktk, vt, vtk, qt, qtk, qrt, qrk = nxt
        if h + 1 < 6:
            nxt = load_head(h + 1)
        for qi in range(NQ):
            q0 = qi * QT
            po, pok = ps_o.next()
            pd, pdk = ps_d.next()
            def s_step(j):
                ps, psk = ps_s.next()
                P.op("pe", lambda e, ps=ps, kt=kt, qt=qt, j=j, q0=q0: e.matmul(ps[:], kt[:, j * 128:(j + 1) * 128], qt[:, q0:q0 + QT], start=True, stop=False),
                     reads=[ktk, qtk], writes=[psk])
                P.op("pe", lambda e, ps=ps, qrt=qrt, j=j, q0=q0: e.matmul(ps[:], ktr[0][:, j * 128:(j + 1) * 128], qrt[:, q0:q0 + QT], start=False, stop=True),
                     reads=[ktr[1], qrk], writes=[psk])
                pt, ptk = pT.next()
                P.op("act", lambda e, pt=pt, ps=ps: e.activation(out=pt[:], in_=ps[:], func=AF.Exp), reads=[psk], writes=[ptk])
                return pt, ptk

            LOOK = 3
            pend = [s_step(j) for j in range(min(LOOK, NKT))]
            for j in range(NKT):
                if j + LOOK < NKT:
                    pend.append(s_step(j + LOOK))
                pt, ptk = pend.pop(0)
                P.op("pe", lambda e, po=po, vt=vt, pt=pt, j=j: e.matmul(po[:], vt[:, j * 128:(j + 1) * 128], pt[:], start=(j == 0), stop=(j == NKT - 1)),
                     reads=[vtk, ptk], writes=[pok])
                P.op("pe", lambda e, pd=pd, pt=pt, j=j: e.matmul(pd[:], ones[0][:], pt[:], start=(j == 0), stop=(j == NKT - 1)),
                     reads=[ones[1], ptk], writes=[pdk])
            finish(po, pok, pd, pdk, y_mla[h, :, q0:q0 + QT])

    P.dma("sp", masks[0], mask_d, writes=[masks[1]] + ([kbuf.items[0][1]] if alias_masks else []), chan=masks[1])
    def load_head_d(h):
        kt, ktk = kbuf.next()
        vt, vtk = vbuf.next()
        qt, qtk = qn.next()
        P.dma("sp", qt[:], QTd_d[h], writes=[qtk])
        HH = DIL_HALO
        P.dma("sp", kt[:, HH:HH + T], KTd_o[h], writes=[ktk])
        P.dma("pool", vt[:, HH:HH + T], Vdb_o[h], writes=[vtk])

        def halo(dst_full, dtk, src_g, lo_full, c0):
            for hf_ in range(4):
                HW = HH // 4
                dst = dst_full[:, hf_ * HW:(hf_ + 1) * HW]
                lo = lo_full + hf_ * HW
                tmp, tmk = htmp.next()
                P.dma("sp", tmp[:], src_g[:, h, :, lo:lo + HW].rearrange("r p t -> p r t"), writes=[tmk])
                P.op("dve", lambda e, dst=dst, tmp=tmp: e.tensor_scalar_mul(out=dst, in0=tmp[:, 0, :], scalar1=sel[0][:, c0:c0 + 1]),
                     reads=[tmk, sel[1]], writes=[dtk])
                for r in range(1, 4):
                    P.op("dve", lambda e, r=r, dst=dst, tmp=tmp: e.scalar_tensor_tensor(out=dst, in0=tmp[:, r, :], scalar=sel[0][:, c0 + r:c0 + r + 1], in1=dst,
                                                                                       op0=ALU.mult, op1=ALU.add), reads=[tmk, sel[1], dtk], writes=[dtk])

        halo(kt[:, 0:HH], ktk, KTd_g, T - HH, 4)
        halo(kt[:, HH + T:TH], ktk, KTd_g, 0, 8)
        halo(vt[:, 0:HH], vtk, Vdb_g, T - HH, 4)
        halo(vt[:, HH + T:TH], vtk, Vdb_g, 0, 8)
        return kt, ktk, vt, vtk, qt, qtk

    nxt = load_head_d(0)
    for h in range(6):
        kt, ktk, vt, vtk, qt, qtk = nxt
        if h + 1 < 6:
            nxt = load_head_d(h + 1)
        for qi in range(NQ):
            q0 = qi * QT
            po, pok = ps_o.next()
            pd, pdk = ps_d.next()
            def d_step(i):
                j = qi * (QT // 128) + i
                ps, psk = ps_s.next()
                P.op("pe", lambda e, ps=ps, kt=kt, qt=qt, j=j, q0=q0: e.matmul(ps[:], kt[:, j * 128:(j + 1) * 128], qt[:, q0:q0 + QT], start=True, stop=True),
                     reads=[ktk, qtk], writes=[psk])
                et, etk = eT.next()
                P.op("act", lambda e, et=et, ps=ps: e.activation(out=et[:], in_=ps[:], func=AF.Exp), reads=[psk], writes=[etk])
                pt, ptk = pT.next()
                P.op("dve", lambda e, pt=pt, et=et, i=i, j=j: e.scalar_tensor_tensor(
                    out=pt[:], in0=et[:], scalar=valid[0][:, j:j + 1], in1=masks[0][:, i, :], op0=ALU.mult, op1=ALU.mult),
                    reads=[etk, valid[1], masks[1]], writes=[ptk])
                return pt, ptk, j

            LOOK = 2
            pend = [d_step(i) for i in range(LOOK)]
            for i in range(DIL_NKT):
                if i + LOOK < DIL_NKT:
                    pend.append(d_step(i + LOOK))
                pt, ptk, j = pend.pop(0)
                P.op("pe", lambda e, po=po, vt=vt, pt=pt, j=j, i=i: e.matmul(po[:], vt[:, j * 128:(j + 1) * 128], pt[:], start=(i == 0), stop=(i == DIL_NKT - 1)),
                     reads=[vtk, ptk], writes=[pok])
                P.op("pe", lambda e, pd=pd, pt=pt, i=i: e.matmul(pd[:], ones[0][:], pt[:], start=(i == 0), stop=(i == DIL_NKT - 1)),
                     reads=[ones[1], ptk], writes=[pdk])
            finish(po, pok, pd, pdk, y_dil[h, :, q0:q0 + QT])
    ctx.end()


def emit_ML(ctx, S):
    nc, P = ctx.nc, ctx.P
    dt_ = nc.dram_tensor
    NJ = S // 128
    T4 = S // 4
    NJc = T4 // 128
    qT_g = ctx.din("qT_g", [4, 4, 64, T4], BF16)
    kT_g = ctx.din("kT_g", [4, 4, 64, T4], BF16)
    kb_g = ctx.din("kb_g", [4, 4, 128, NJc, 64], BF16)
    vb_g = ctx.din("vb_g", [4, 4, 128, NJc, 128], BF16)
    osb_g = ctx.din("osb_g", [4, 4, 2, 64, NJc, 128], F32)
    gb_g = ctx.din("gb_g", [4, 128, NJc, 16], F32)
    sel_d = ctx.din("sel", [128, 12], F32)
    bias_d = ctx.din("bias", [128, 4], F32)
    gain_d = ctx.din("gain", [128, 128], F32)
    tri_d = ctx.din("tri", [128, 3, 128], F32)
    ya_d = ctx.dout("ya", [4, 128, S // 512, 128], BF16)

    def sb(shape, dt, name):
        return ctx.sb(name, shape, dt)

    def ring(n, shape, dt, name):
        return Ring([(sb(shape, dt, f"{name}{i}"), P.tok(f"{name}{i}")) for i in range(n)])

    def psring(n, shape, dt, name):
        return Ring([(ctx.ps(f"{name}{i}", shape, dt), P.tok(f"{name}{i}")) for i in range(n)])

    qT = (sb([64, S], BF16, "qT"), P.tok())
    kT = (sb([64, S], BF16, "kT"), P.tok())
    kb = (sb([128, NJ, 64], BF16, "kb"), P.tok())
    vb = (sb([128, NJ, 128], BF16, "vb"), P.tok())
    gb = (sb([128, NJ, 4], F32, "gb"), P.tok())
    bias = (sb([128, 4], F32, "bias"), P.tok())
    gain = (sb([128, 128], F32, "gain"), P.tok())
    tri = (sb([128, 3, 128], F32, "tri"), P.tok())
    trib = (sb([128, 2, 128], BF16, "trib"), P.tok())
    hf = (sb([128, NJ, 128], BF16, "hf"), P.toks(NJ))
    lf = [(sb([128, NJ], F32, f"lf{d}"), P.tok()) for d in range(2)]
    ea = [(sb([128, NJ], F32, f"ea{d}"), P.tok()) for d in range(2)]
    eb = [(sb([128, NJ], F32, f"eb{d}"), P.tok()) for d in range(2)]
    et = [(sb([128, NJ], F32, f"et{d}"), P.tok()) for d in range(2)]
    C32 = [(sb([64, 130], F32, f"C32_{d}"), P.tok()) for d in range(2)]
    Cb = [ring(2, [64, 130], BF16, f"Cb{d}") for d in range(2)]
    vx = ring(3, [128, 130], BF16, "vx")
    sm = ring(3, [128, 128], BF16, "sm")
    o32 = ring(3, [128, 130], F32, "o32")
    st = ring(4, [128, 4], F32, "st")
    hs = ring(2, [128, 128], F32, "hs")
    sq = ring(2, [128, 128], F32, "sqm")
    og = ring(2, [128, 128], F32, "og")
    og4 = (sb([128, 512], F32, "og4"), P.tok())
    yo = ring(3, [128, 128], BF16, "yo")
    ps_g = psring(2, [128, 512], F32, "ps_g")
    ps_s = psring(2, [128, 128], F32, "ps_s")
    ps_o = psring(2, [128, 130], F32, "ps_o")
    ps_u = psring(2, [64, 130], F32, "ps_u")
    outs = []

    sel = (sb([128, 12], F32, "sel"), P.tok())
    gfull = (sb([128, NJ, 16], F32, "gfull"), P.tok())
    stmp = ring(1, [128, 4, 1024], BF16, "stmp")
    stmp32 = ring(1, [128, 4, 512], F32, "stmp32")
    P.dma("sp", sel[0][:], sel_d, writes=[sel[1]])

    def select(dst, dtk, tmp, tmk, np_=128):
        P.op("dve", lambda e: e.tensor_scalar_mul(out=dst, in0=tmp[0:np_, 0], scalar1=sel[0][0:np_, 0:1]), reads=[tmk, sel[1]], writes=[dtk])
        for h in range(1, 4):
            P.op("dve", lambda e, h=h: e.scalar_tensor_tensor(out=dst, in0=tmp[0:np_, h], scalar=sel[0][0:np_, h:h + 1], in1=dst,
                                                               op0=ALU.mult, op1=ALU.add), reads=[tmk, sel[1], dtk], writes=[dtk])

    for r in range(4):
        for c in range(T4 // 1024):
            for (dstt, srcg) in ((qT, qT_g), (kT, kT_g)):
                tmp, tmk = stmp.next()
                P.dma("sp", tmp[0:64], srcg[r, :, :, c * 1024:(c + 1) * 1024].rearrange("h p t -> p h t"), writes=[tmk])
                select(dstt[0][:, r * T4 + c * 1024:r * T4 + (c + 1) * 1024], dstt[1], tmp, tmk, 64)
        for c in range(NJc // 16):
            tmp, tmk = stmp.next()
            tv = tmp[:].rearrange("p h (j d) -> p h j d", d=64)
            P.dma("pool", tv, kb_g[r, :, :, c * 16:(c + 1) * 16, :].rearrange("h p j d -> p h j d"), writes=[tmk])
            select(kb[0][:, r * NJc + c * 16:r * NJc + (c + 1) * 16, :], kb[1], tv, tmk)
        for c in range(NJc // 8):
            tmp, tmk = stmp.next()
            tv = tmp[:].rearrange("p h (j d) -> p h j d", d=128)
            P.dma("pool", tv, vb_g[r, :, :, c * 8:(c + 1) * 8, :].rearrange("h p j d -> p h j d"), writes=[tmk])
            select(vb[0][:, r * NJc + c * 8:r * NJc + (c + 1) * 8, :], vb[1], tv, tmk)
        P.dma("sp", gfull[0][:, r * NJc:(r + 1) * NJc, :], gb_g[r], writes=[gfull[1]])
    for k in range(4):
        base = (k // 2) * 8 + (k % 2) * 4
        P.op("dve", lambda e, k=k, base=base: e.tensor_scalar_mul(out=gb[0][:, :, k], in0=gfull[0][:, :, base], scalar1=sel[0][:, 0:1]),
             reads=[gfull[1], sel[1]], writes=[gb[1]])
        for h in range(1, 4):
            P.op("dve", lambda e, k=k, base=base, h=h: e.scalar_tensor_tensor(out=gb[0][:, :, k], in0=gfull[0][:, :, base + h], scalar=sel[0][:, h:h + 1],
                                                                             in1=gb[0][:, :, k], op0=ALU.mult, op1=ALU.add),
                 reads=[gfull[1], sel[1], gb[1]], writes=[gb[1]])
    P.dma("sp", bias[0][:], bias_d, writes=[bias[1]])
    P.dma("sp", gain[0][:], gain_d, writes=[gain[1]])
    P.dma("sp", tri[0][:], tri_d, writes=[tri[1]])
    P.op("dve", lambda e: e.tensor_copy(out=trib[0][:], in_=tri[0][:, 0:2, :]), reads=[tri[1]], writes=[trib[1]])

    for d in range(2):
        gi = gb[0][:, :, 2 * d]
        gf = gb[0][:, :, 2 * d + 1]
        lft, lfk = lf[d]
        eat, eak = ea[d]
        ebt, ebk = eb[d]
        ett, etk = et[d]
        P.op("dve", lambda e, lft=lft, gf=gf, d=d: e.tensor_scalar(out=lft[:], in0=gf, scalar1=bias[0][:, 2 * d + 1:2 * d + 2], scalar2=-1.0,
                                                                   op0=ALU.add, op1=ALU.mult), reads=[gb[1], bias[1]], writes=[lfk])
        P.op("act", lambda e, lft=lft: e.activation(out=lft[:], in_=lft[:], func=AF.Exp), reads=[lfk], writes=[lfk])
        P.op("dve", lambda e, lft=lft: e.tensor_scalar_add(out=lft[:], in0=lft[:], scalar1=1.0), reads=[lfk], writes=[lfk])
        P.op("act", lambda e, lft=lft: e.activation(out=lft[:], in_=lft[:], func=AF.Ln), reads=[lfk], writes=[lfk])
        P.op("dve", lambda e, lft=lft: e.tensor_scalar_mul(out=lft[:], in0=lft[:], scalar1=-1.0), reads=[lfk], writes=[lfk])
        pg, pgk = ps_g.next()
        P.op("pe", lambda e, pg=pg, lft=lft, d=d: e.matmul(pg[:, 0:NJ], tri[0][:, d, :], lft[:], start=True, stop=True),
             reads=[tri[1], lfk], writes=[pgk])
        P.op("pe", lambda e, pg=pg, lft=lft: e.matmul(pg[:, 256:256 + NJ], tri[0][:, 2, :], lft[:], start=True, stop=True),
             reads=[tri[1], lfk], writes=[pgk])
        P.op("dve", lambda e, eat=eat, gi=gi, pg=pg, d=d: e.scalar_tensor_tensor(
            out=eat[:], in0=gi, scalar=bias[0][:, 2 * d:2 * d + 1], in1=pg[:, 0:NJ], op0=ALU.add, op1=ALU.subtract),
            reads=[gb[1], bias[1], pgk], writes=[eak])
        P.op("act", lambda e, eat=eat: e.activation(out=eat[:], in_=eat[:], func=AF.Exp), reads=[eak], writes=[eak])
        P.op("act", lambda e, ebt=ebt, pg=pg: e.activation(out=ebt[:], in_=pg[:, 0:NJ], func=AF.Exp), reads=[pgk], writes=[ebk])
        P.op("dve", lambda e, ebt=ebt: e.tensor_scalar_mul(out=ebt[:], in0=ebt[:], scalar1=0.125), reads=[ebk], writes=[ebk])
        P.op("act", lambda e, ett=ett, pg=pg: e.activation(out=ett[:], in_=pg[:, 256:256 + NJ], func=AF.Exp), reads=[pgk], writes=[etk])
        P.op("pool", lambda e, d=d: e.memset(C32[d][0][:], 0.0), writes=[C32[d][1]])

    def chunk(d, j, first):
        eat, eak = ea[d]
        ebt, ebk = eb[d]
        ett, etk = et[d]
        c0 = j * 128
        ps, psk = ps_s.next()
        P.op("pe", lambda e: e.matmul(ps[:], kT[0][:, c0:c0 + 128], qT[0][:, c0:c0 + 128], start=True, stop=True),
             reads=[kT[1], qT[1]], writes=[psk])
        smt, smk = sm.next()
        P.op("dve", lambda e: e.tensor_tensor(out=smt[:], in0=ps[:], in1=trib[0][:, d, :], op=ALU.mult), reads=[psk, trib[1]], writes=[smk])
        vxt, vxk = vx.next()
        P.op("pool", lambda e: e.tensor_scalar_mul(out=vxt[:, 0:128], in0=vb[0][:, j, :], scalar1=eat[:, j:j + 1]), reads=[vb[1], eak], writes=[vxk])
        P.op("pool", lambda e: e.tensor_copy(out=vxt[:, 128:129], in_=eat[:, j:j + 1]), reads=[eak], writes=[vxk])
        po, pok = ps_o.next()
        P.op("pe", lambda e: e.matmul(po[:, 0:129], smt[:], vxt[:, 0:129], start=True, stop=first), reads=[smk, vxk], writes=[pok])
        if not first:
            cbt, cbk = Cb[d].items[(Cb[d].i - 1) % 2]
            P.op("pe", lambda e: e.matmul(po[:, 0:129], qT[0][:, c0:c0 + 128], cbt[:, 0:129], start=False, stop=True),
                 reads=[qT[1], cbk], writes=[pok])
        pu, puk = ps_u.next()
        P.op("pe", lambda e: e.matmul(pu[:, 0:129], kb[0][:, j, :], vxt[:, 0:129], start=True, stop=True), reads=[kb[1], vxk], writes=[puk])
        c32, c32k = C32[d]
        P.op("dve", lambda e: e.tensor_tensor(out=c32[:, 0:129], in0=c32[:, 0:129], in1=pu[:, 0:129], op=ALU.add), reads=[c32k, puk], writes=[c32k])
        P.op("dve", lambda e: e.tensor_scalar_mul(out=c32[:, 0:129], in0=c32[:, 0:129], scalar1=ett[0:64, j:j + 1]), reads=[c32k, etk], writes=[c32k])
        cbn, cbnk = Cb[d].next()
        P.op("act", lambda e: e.activation(out=cbn[:, 0:129], in_=c32[:, 0:129], func=AF.Copy), reads=[c32k], writes=[cbnk])
        ot, otk = o32.next()
        P.op("act", lambda e: e.activation(out=ot[:, 0:129], in_=po[:, 0:129], func=AF.Copy, scale=ebt[:, j:j + 1]), reads=[pok, ebk], writes=[otk])
        s4, s4k = st.next()
        P.op("dve", lambda e: e.scalar_tensor_tensor(out=s4[:, 0:1], in0=ot[:, 128:129], scalar=-1.0, in1=ot[:, 128:129], op0=ALU.mult, op1=ALU.max),
             reads=[otk], writes=[s4k])
        P.op("dve", lambda e: e.tensor_scalar_max(out=s4[:, 0:1], in0=s4[:, 0:1], scalar1=1.0), reads=[s4k], writes=[s4k])
        P.op("dve", lambda e: e.reciprocal(out=s4[:, 0:1], in_=s4[:, 0:1]), reads=[s4k], writes=[s4k])
        return ot, otk, s4, s4k

    for j in range(NJ):
        ot, otk, s4, s4k = chunk(0, j, j == 0)
        P.op("dve", lambda e, ot=ot, s4=s4, j=j: e.tensor_scalar_mul(out=hf[0][:, j, :], in0=ot[:, 0:128], scalar1=s4[:, 0:1]),
             reads=[otk, s4k], writes=[hf[1][j]])
    for jj in range(NJ):
        j = NJ - 1 - jj
        ot, otk, s4, s4k = chunk(1, j, jj == 0)
        ht, htk = hs.next()
        P.op("dve", lambda e, ot=ot, s4=s4, j=j, ht=ht: e.scalar_tensor_tensor(out=ht[:], in0=ot[:, 0:128], scalar=s4[:, 0:1], in1=hf[0][:, j, :],
                                                                               op0=ALU.mult, op1=ALU.add), reads=[otk, s4k, hf[1][j]], writes=[htk])
        sqt, sqk = sq.next()
        P.op("act", lambda e, sqt=sqt, ht=ht, s4=s4: e.activation(out=sqt[:], in_=ht[:], func=AF.Square, scale=128.0 ** -0.5, accum_out=s4[:, 1:2]),
             reads=[htk], writes=[sqk, s4k])
        P.op("dve", lambda e, s4=s4: e.tensor_scalar_add(out=s4[:, 1:2], in0=s4[:, 1:2], scalar1=EPS), reads=[s4k], writes=[s4k])
        P.op("act", lambda e, s4=s4: e.activation(out=s4[:, 1:2], in_=s4[:, 1:2], func=AF.Ln), reads=[s4k], writes=[s4k])
        P.op("act", lambda e, s4=s4: e.activation(out=s4[:, 1:2], in_=s4[:, 1:2], func=AF.Exp, scale=-0.5), reads=[s4k], writes=[s4k])
        ogt, ogk = og.next()
        if jj % 4 == 0:
            j_lo = j - 3
            rr, jl = j_lo // NJc, j_lo % NJc
            t32, t32k = stmp32.next()
            tv32 = t32[:].rearrange("p h (j d) -> p h j d", d=128)
            for c2 in range(2):
                P.dma("sp", tv32[c2 * 64:(c2 + 1) * 64], osb_g[rr, :, c2, :, jl:jl + 4, :].rearrange("h p j d -> p h j d"), writes=[t32k])
            select(og4[0][:].rearrange("p (j d) -> p j d", d=128), og4[1], tv32, t32k)
        P.op("act", lambda e, ogt=ogt, j=j: e.activation(out=ogt[:], in_=og4[0][:, ((j % 4)) * 128:((j % 4) + 1) * 128], func=AF.Copy),
             reads=[og4[1]], writes=[ogk])
        P.op("pool", lambda e, ogt=ogt: e.tensor_tensor(out=ogt[:], in0=ogt[:], in1=gain[0][:], op=ALU.mult), reads=[ogk, gain[1]], writes=[ogk])
        yt, ytk = yo.next()
        P.op("dve", lambda e, yt=yt, ht=ht, s4=s4, ogt=ogt: e.scalar_tensor_tensor(out=yt[:], in0=ht[:], scalar=s4[:, 1:2], in1=ogt[:],
                                                                                   op0=ALU.mult, op1=ALU.mult), reads=[htk, s4k, ogk], writes=[ytk])
        P.dma("pool", ya_d[j // NJc, :, j % NJc, :], yt[:], reads=[ytk], writes=[], chan=ytk)
        if ytk not in outs:
            outs.append(ytk)
    ctx.end()


def emit_YA(ctx, T, S):
    nc, P = ctx.nc, ctx.P
    NJ = S // 128
    NJc = T // 128
    ya_g = ctx.din("ya_g", [4, 128, 4, T], BF16)
    sel_d = ctx.din("sel", [128, 12], F32)
    ident_d = ctx.din("ident", [128, 128], BF16)
    yT = ctx.dout("yT", [128, 16, T], BF16)
    sel = (ctx.sb("sel", [128, 12], F32), P.tok())
    ident = (ctx.sb("ident", [128, 128], BF16), P.tok())
    tmp = Ring([(ctx.sb(f"t{i}", [128, 4, 1024], BF16), P.tok()) for i in range(2)])
    sl = Ring([(ctx.sb(f"sl{i}", [128, 8, 128], BF16), P.tok()) for i in range(2)])
    ob = Ring([(ctx.sb(f"ob{i}", [128, 1024], BF16), P.tok()) for i in range(2)])
    pt = Ring([(ctx.ps(f"pt{i}", [128, 1024], BF16), P.tok()) for i in range(2)])
    P.dma("sp", sel[0][:], sel_d, writes=[sel[1]])
    P.dma("sp", ident[0][:], ident_d, writes=[ident[1]])
    for h in range(4):
        for c in range(NJc // 8):
            t, tk = tmp.next()
            tv = t[:].rearrange("p s (j d) -> p s j d", d=128)
            P.dma("sp", t[:], ya_g[h][:, :, c * 1024:(c + 1) * 1024], writes=[tk])
            st, stk = sl.next()
            P.op("dve", lambda e, st=st, tv=tv: e.tensor_scalar_mul(out=st[:], in0=tv[:, 0], scalar1=sel[0][:, 0:1]), reads=[tk, sel[1]], writes=[stk])
            for r in range(1, 4):
                P.op("dve", lambda e, st=st, tv=tv, r=r: e.scalar_tensor_tensor(out=st[:], in0=tv[:, r], scalar=sel[0][:, r:r + 1], in1=st[:],
                                                                                op0=ALU.mult, op1=ALU.add), reads=[tk, sel[1], stk], writes=[stk])
            p_, pk = pt.next()
            for j in range(8):
                P.op("pe", lambda e, p_=p_, st=st, j=j: e.transpose(p_[:, j * 128:(j + 1) * 128], st[:, j, :], ident[0][:]),
                     reads=[stk, ident[1]], writes=[pk])
            o_, ok = ob.next()
            P.op("act", lambda e, o_=o_, p_=p_: e.activation(out=o_[:], in_=p_[:], func=AF.Copy), reads=[pk], writes=[ok])
            P.dma("pool", yT[:, h, c * 1024:(c + 1) * 1024], o_[:], reads=[ok], writes=[], chan=ok)
    ctx.end()


XCH = (("KTn", 768), ("Vb", 768), ("KTd", 768), ("Vdb", 768), ("qT", 512), ("kT", 512), ("vb", 512), ("kb", 512), ("KTr", 128))
XCH_ROWS = sum(n for _, n in XCH)
W_PIECE = 128 * 4096


def build_fused(S, NI, debug=False):
    T = S // 4
    assert T == 4096
    NJ, NJc = S // 128, T // 128
    nc = bass.Bass("TRN2", target_bir_lowering=False)
    P = Prog(nc)
    G4 = [[0, 1, 2, 3], [4, 5, 6, 7]]

    def ext_in(name, shape, dt):
        return nc.dram_tensor(name, list(shape), dt, kind="ExternalInput").ap()

    def scratch(name, shape, dt):
        return nc.dram_tensor(name, list(shape), dt).ap()

    def gather(src2d, dst2d):
        P.collective("AllGather", src2d.opt(), dst2d.opt(), G4)

    wsrc = ext_in("wsrc", [NI * 128, 4096], F32)
    xT_in = ext_in("xT", [128, 16, T], F32)
    rope_in = ext_in("rope", [T, 96], F32)
    ident_in = ext_in("ident", [128, 128], BF16)
    masks_in = ext_in("masks", [128, DIL_NKT, 512], BF16)
    valid_in = ext_in("valid", [128, (T + 2 * DIL_HALO) // 128], F32)
    sel_in = ext_in("sel", [128, 12], F32)
    tri_in = ext_in("tri", [128, 3, 128], F32)
    lay = []
    for l in range(DEPTH):
        lay.append(dict(
            gmix=ext_in(f"gmix{l}", [128, 16], F32), wq=ext_in(f"wq{l}", [128, 3, 1152], F32), wkv=ext_in(f"wkv{l}", [128, 1536], F32),
            qlat_g=ext_in(f"qlat_g{l}", [128, 3], F32), kvlat_g=ext_in(f"kvlat_g{l}", [128, 1], F32), gains=ext_in(f"gains{l}", [128, 704], F32),
            bias=ext_in(f"mlbias{l}", [128, 4], F32), gain=ext_in(f"mlgain{l}", [128, 128], F32), gff=ext_in(f"gff{l}", [128, 16], F32)))
    outT = nc.dram_tensor("outT", [128, 16, T], F32, kind="ExternalOutput").ap()

    wpart = scratch("wpart", [NI * 128, 4096], BF16)
    wall = scratch("wall", [NI * 512, 4096], BF16)
    ctx = Ctx(nc, P, bind={"src": wsrc.rearrange("(i p) c -> p i c", p=128), "dst": wpart.rearrange("(i p) c -> p i c", p=128)}, tag="w_")
    emit_cast(ctx, NI)
    n_first = -(-(D_MODEL * D_IN) // (4 * W_PIECE))
    for i in range(n_first):
        gather(wpart[i * 128:(i + 1) * 128], wall[i * 512:(i + 1) * 512])
    for i in range(n_first, NI):
        gather(wpart[i * 128:(i + 1) * 128], wall[i * 512:(i + 1) * 512])
    P.barrier()
    wflat = wall.rearrange("a b -> (a b)")
    szs = [D_MODEL * D_IN, D_MODEL * D_MODEL, D_MODEL * 4 * D_MODEL, 4 * D_MODEL * D_MODEL]
    off = 0
    for l in range(DEPTH):
        lay[l]["win"] = wflat[off:off + szs[0]].rearrange("(p k n) -> p k n", p=128, k=16)
        off += szs[0]
        lay[l]["wout_b"] = wflat[off:off + szs[1]].rearrange("(c p n) -> c p n", c=16, p=128)
        off += szs[1]
        lay[l]["w1_b"] = wflat[off:off + szs[2]].rearrange("(c p n) -> c p n", c=64, p=128)
        off += szs[2]
        lay[l]["w2_b"] = wflat[off:off + szs[3]].rearrange("(c p n) -> c p n", c=64, p=128)
        off += szs[3]

    NCH = XCH_ROWS // 128
    xch = scratch("xch", [XCH_ROWS, T], BF16)
    xg = scratch("xg", [NCH, 4, 128, T], BF16)
    osb_o = scratch("osb_o", [512, T], F32)
    osb_gs = scratch("osb_gs", [8, 4, 64, T], F32)
    gb_o = scratch("gb_o", [16, T], F32)
    gb_gs = scratch("gb_gs", [4 * 16, T], F32)
    QTn = scratch("QTn", [6, 128, T], BF16)
    QTr = scratch("QTr", [6, 64, T], BF16)
    QTd = scratch("QTd", [6, 128, T], BF16)
    ya_o = scratch("ya_o", [4, 128, T], BF16)
    ya_gs = scratch("ya_gs", [4, 4 * 128, T], BF16)
    if debug:
        yT = nc.dram_tensor("dbg_yT", [128, 16, T], BF16, kind="ExternalOutput").ap()
        x1T = nc.dram_tensor("dbg_x1T", [128, 16, T], F32, kind="ExternalOutput").ap()
        yT1 = scratch("yT1", [128, 16, T], BF16)
    else:
        yT = scratch("yT", [128, 16, T], BF16)
        x1T = scratch("x1T", [128, 16, T], F32)
        yT1 = yT
    R = {}
    a = 0
    for k, n in XCH:
        R[k] = (a, a + n)
        a += n

    def own(k):
        a, b = R[k]
        return xch[a:b]

    def gat(k):
        a, b = R[k]
        return xg[a // 128:b // 128]

    def hp(ap):
        return ap.rearrange("(h p) t -> h p t", p=128)

    def ghp(k):
        return gat(k).rearrange("h r p t -> r h p t")

    for l in range(DEPTH):
        L = lay[l]
        x_in = xT_in if l == 0 else x1T
        x_out = x1T if l == 0 else outT
        yT_l = yT if l == 0 else yT1
        bindA = dict(xT=x_in, win=L["win"], gmix=L["gmix"], wq=L["wq"], wkv=L["wkv"], qlat_g=L["qlat_g"], kvlat_g=L["kvlat_g"], gains=L["gains"],
                     rope=rope_in, ident=ident_in,
                     o_qTml=hp(own("qT"))[:, 0:64, :], o_kTml=hp(own("kT"))[:, 0:64, :],
                     o_kb=hp(own("kb"))[:, :, 0:NJc * 64].rearrange("h p (j d) -> h p j d", d=64),
                     o_vb=hp(own("vb")).rearrange("h p (j d) -> h p j d", d=128),
                     o_osb=hp(osb_o).rearrange("h p (j d) -> h p j d", d=128),
                     o_gb=gb_o.rearrange("q t -> (q t)").rearrange("(p j d) -> p j d", p=128, d=16),
                     o_QTn=QTn, o_QTr=QTr, o_KTn=hp(own("KTn")), o_KTr=own("KTr")[0:64],
                     o_Vb=hp(own("Vb")), o_QTd=QTd, o_KTd=hp(own("KTd")), o_Vdb=hp(own("Vdb")))
        emit_A(Ctx(nc, P, bind=bindA, tag=f"a{l}_"), T)
        ml_chunks = [i for k in ("qT", "kT", "vb", "kb") for i in range(R[k][0] // 128, R[k][1] // 128)]
        for i in ml_chunks:
            gather(xch[i * 128:(i + 1) * 128], xg[i].rearrange("r p t -> (r p) t"))
        for i in range(8):
            gather(osb_o[i * 64:(i + 1) * 64], osb_gs[i].rearrange("r p t -> (r p) t"))
        gather(gb_o, gb_gs)
        for i in range(NCH):
            if i not in ml_chunks:
                gather(xch[i * 128:(i + 1) * 128], xg[i].rearrange("r p t -> (r p) t"))
        P.barrier()
        bindM = dict(qT_g=ghp("qT")[:, :, 0:64, :], kT_g=ghp("kT")[:, :, 0:64, :],
                     kb_g=ghp("kb")[:, :, :, 0:NJc * 64].rearrange("r h p (j d) -> r h p j d", d=64),
                     vb_g=ghp("vb").rearrange("r h p (j d) -> r h p j d", d=128),
                     osb_g=osb_gs.rearrange("(h c) r q (j d) -> r h c q j d", c=2, d=128),
                     gb_g=gb_gs.rearrange("(r q) t -> r (q t)", r=4).rearrange("r (p j d) -> r p j d", p=128, d=16),
                     sel=sel_in, bias=L["bias"], gain=L["gain"], tri=tri_in, ya=ya_o.rearrange("s p (j d) -> s p j d", d=128))
        emit_ML(Ctx(nc, P, bind=bindM, tag=f"m{l}_"), S)
        for sg in range(4):
            gather(ya_o[sg], ya_gs[sg])
        bindT = dict(QTn=QTn, QTr=QTr, QTd=QTd, KTn_g=ghp("KTn"), KTr_g=gat("KTr")[0][:, 0:64, :], Vb_g=ghp("Vb"),
                     KTd_o=hp(own("KTd")), Vdb_o=hp(own("Vdb")), KTd_g=ghp("KTd"), Vdb_g=ghp("Vdb"),
                     sel=sel_in, valid=valid_in, masks=masks_in,
                     y_mla=yT_l.rearrange("p k t -> k p t")[4:10], y_dil=yT_l.rearrange("p k t -> k p t")[10:16])
        emit_ATT(Ctx(nc, P, bind=bindT, tag=f"t{l}_"), T, S)
        emit_YA(Ctx(nc, P, bind=dict(ya_g=ya_gs.rearrange("s (h p) n -> h p s n", p=128), sel=sel_in, ident=ident_in, yT=yT_l), tag=f"y{l}_"), T, S)
        bindC = dict(xT=x_in, yT=yT_l, wout_b=L["wout_b"], w1_b=L["w1_b"], w2_b=L["w2_b"], gff=L["gff"], xoT=x_out)
        emit_C(Ctx(nc, P, bind=bindC, tag=f"c{l}_"), T)
    P.finalize()
    P.emit()
    return nc, P


def _rope_tab(seq, rot):
    pos = np.arange(seq, dtype=np.float32)
    inv = (np.float32(500000.0) ** (-np.arange(0, rot, 2, dtype=np.float32) / np.float32(rot))).astype(np.float32)
    ang = (pos[:, None] * inv[None, :]).astype(np.float32)
    return np.cos(ang).astype(np.float32), np.sin(ang).astype(np.float32)


def _blk(w):
    Kd, N = w.shape
    return np.ascontiguousarray(w.reshape(Kd // 128, 128, N // 128, 128).transpose(2, 1, 0, 3)).reshape(N // 128, 128, Kd)


def _to_bf16_exact(a):
    import ml_dtypes
    u = np.ascontiguousarray(a, np.float32).view(np.uint32)
    return (u >> 16).astype(np.uint16).view(ml_dtypes.bfloat16)


_DEBUG = {"on": True}


def kernel(x, norm_mix, w_in, ml_i_bias, ml_f_bias, ml_out_norm, mla_q_norm, mla_w_q_b, mla_kv_norm, mla_w_kv_b,
           mla_q_head_norm, mla_k_head_norm, dil_q_norm, dil_k_norm, w_out, norm_ff, w_ff1, w_ff2):
    f32 = np.float32
    x = np.asarray(x, f32)
    B, S, D = x.shape
    T = S // 4
    parts = []
    for l in range(DEPTH):
        wi = np.asarray(w_in[l], f32)
        parts.append(np.ascontiguousarray(wi.reshape(16, 128, D_IN).transpose(1, 0, 2)).reshape(-1))
        parts.append(_blk(np.asarray(w_out[l], f32)).reshape(-1))
        parts.append(_blk(np.asarray(w_ff1[l], f32)).reshape(-1))
        w2 = np.asarray(w_ff2[l], f32)
        parts.append(np.concatenate([_blk(w2[q * D:(q + 1) * D]) for q in range(4)], 0).reshape(-1))
    flat = np.concatenate(parts)
    del parts
    NI = -(-flat.size // (4 * W_PIECE))
    pad = NI * 4 * W_PIECE - flat.size
    if pad:
        flat = np.concatenate([flat, np.zeros(pad, f32)])
    flat = flat.reshape(NI, 4, 128, 4096)
    cb, sb_ = _rope_tab(S, 64)
    cc_, sc_ = _rope_tab(S, 32)
    rope_all = np.concatenate([cb, sb_, cc_, sc_], 1).astype(f32)
    tri = np.zeros((128, 3, 128), f32)
    s_ = np.arange(128)[:, None]
    t_ = np.arange(128)[None, :]
    tri[:, 0] = (s_ <= t_)
    tri[:, 1] = (s_ >= t_)
    tri[:, 2] = 1.0
    ident_b = _to_bf16_exact(np.eye(128, dtype=f32))
    masks_b = _to_bf16_exact(np.ascontiguousarray(dil_masks().transpose(1, 0, 2)))
    H = DIL_HALO
    in_maps = []
    for c in range(NCORES):
        b, r = c // 4, c % 4
        tok = np.arange(r * T - H, (r + 1) * T + H)
        sel = np.zeros((128, 12), f32)
        sel[:, r] = 1.0
        if r > 0:
            sel[:, 4 + r - 1] = 1.0
        if r < 3:
            sel[:, 8 + r + 1] = 1.0
        m = {"wsrc": flat[:, r].reshape(NI * 128, 4096),
             "xT": x[b, r * T:(r + 1) * T].T.reshape(16, 128, T).transpose(1, 0, 2),
             "rope": rope_all[r * T:(r + 1) * T], "ident": ident_b, "masks": masks_b,
             "valid": ((tok >= 0) & (tok < S)).astype(f32).reshape(-1, 128).T, "sel": sel, "tri": tri}
        for l in range(DEPTH):
            gains = np.zeros((128, 704), f32)
            gains[:, 0:192] = mla_q_head_norm[l]
            gains[:, 192:384] = mla_k_head_norm[l]
            gains[:, 384:512] = dil_q_norm[l]
            gains[:, 512:640] = dil_k_norm[l]
            h = r
            bias = np.array([ml_i_bias[l][0, h], ml_f_bias[l][0, h], ml_i_bias[l][1, h], ml_f_bias[l][1, h]], f32)
            m.update({f"gmix{l}": np.asarray(norm_mix[l], f32).reshape(16, 128).T,
                      f"wq{l}": np.asarray(mla_w_q_b[l], f32).reshape(3, 128, 1152).transpose(1, 0, 2),
                      f"wkv{l}": np.asarray(mla_w_kv_b[l], f32), f"qlat_g{l}": np.asarray(mla_q_norm[l], f32).reshape(3, 128).T,
                      f"kvlat_g{l}": np.asarray(mla_kv_norm[l], f32).reshape(128, 1), f"gains{l}": gains,
                      f"mlbias{l}": np.tile(bias[None], (128, 1)), f"mlgain{l}": np.tile(np.asarray(ml_out_norm[l][h], f32)[None], (128, 1)),
                      f"gff{l}": np.asarray(norm_ff[l], f32).reshape(16, 128).T})
        in_maps.append({k: np.ascontiguousarray(v) for k, v in m.items()})
    nc, _ = build_fused(S, NI, debug=_DEBUG["on"])
    res = run_bass_kernel_spmd(nc, in_maps, core_ids=list(range(NCORES))).results
    _DEBUG["res"] = res
    out = np.empty((B, S, D), f32)
    for c in range(NCORES):
        out[c // 4, (c % 4) * T:(c % 4 + 1) * T] = res[c]["outT"].transpose(1, 0, 2).reshape(D, T).T
    return out
```
